# Optimizing a Trainium2 kernel written in Bass

```python
import math
import jax, jax.numpy as jnp
from jax import lax
import numpy as np

D_MODEL = 1024
BATCH = 2
SEQ = 8192
DEPTH = 2

W_SSM = 256
W_RWKV = 256
W_CONV = 256
W_FFT = 256
D_MIX = W_SSM + W_RWKV + W_CONV + W_FFT
D_IN_PROJ = W_SSM + 3 * W_RWKV + 3 * W_CONV + W_FFT
SSM_CH = 16
SSM_GROUPS = W_SSM // SSM_CH
SSM_STATE = 64
DT_MIN = 1e-3
DT_MAX = 1e-1
RWKV_HEAD = 64
RWKV_HEADS = W_RWKV // RWKV_HEAD
DECAY_LORA = 64
AAA_LORA = 64
GATE_LORA = 160
RWKV_DECAY_SCALE = math.exp(-0.5)
RWKV_GN_EPS = 64e-5
CONV_WIDTH = 3
FFT_GROUPS = 4
FFT_CH = W_FFT // FFT_GROUPS
D_FF = int(math.ceil(8 * D_MODEL / 3 / 256)) * 256
DEEPNORM_ALPHA = (2 * DEPTH) ** 0.25
DEEPNORM_BETA = (8 * DEPTH) ** -0.25
LN_EPS = 1e-5

kernel_name = 'hybrid_parallel_head_encoder'


def _layer_norm(x, g, b):
    xf = x.astype(jnp.float32)
    mu = jnp.mean(xf, -1, keepdims=True)
    var = jnp.mean(jnp.square(xf - mu), -1, keepdims=True)
    return ((xf - mu) * lax.rsqrt(var + LN_EPS) * g + b).astype(x.dtype)


def _shift_prev(z):
    return jnp.pad(z[:, :-1], ((0, 0), (1, 0), (0, 0)))


def _shift_next(z):
    return jnp.pad(z[:, 1:], ((0, 0), (0, 1), (0, 0)))


def _flip_backward(z):
    return jnp.concatenate([z[:1], jnp.flip(z[1:], axis=2)], axis=0)


def _cplx_affine_combine(earlier, later):
    a1r, a1i, b1r, b1i = earlier
    a2r, a2i, b2r, b2i = later
    return (a2r * a1r - a2i * a1i,
            a2r * a1i + a2i * a1r,
            a2r * b1r - a2i * b1i + b2r,
            a2r * b1i + a2i * b1r + b2i)


def _s5_mixer(u, lam_re, lam_im, log_dt, b_re, b_im, c_re, c_im, d_skip, glu_w, glu_b):
    bsz, seq, _ = u.shape
    ug = u.reshape(bsz, seq, SSM_GROUPS, SSM_CH)
    dt = jnp.exp(log_dt)[..., None]
    mag = jnp.exp(lam_re * dt)
    lb_re = mag * jnp.cos(lam_im * dt)
    lb_im = mag * jnp.sin(lam_im * dt)
    den = lam_re * lam_re + lam_im * lam_im
    nr = lb_re - 1.0
    coef_re = (nr * lam_re + lb_im * lam_im) / den
    coef_im = (lb_im * lam_re - nr * lam_im) / den
    bb_re = coef_re[..., None] * b_re - coef_im[..., None] * b_im
    bb_im = coef_re[..., None] * b_im + coef_im[..., None] * b_re
    bu_re = _flip_backward(jnp.einsum('bsgh,dgph->dbsgp', ug, bb_re))
    bu_im = _flip_backward(jnp.einsum('bsgh,dgph->dbsgp', ug, bb_im))
    a_re = jnp.broadcast_to(lb_re[:, None, None], bu_re.shape)
    a_im = jnp.broadcast_to(lb_im[:, None, None], bu_re.shape)
    _, _, x_re, x_im = lax.associative_scan(_cplx_affine_combine, (a_re, a_im, bu_re, bu_im), axis=2)
    x_re = _flip_backward(x_re)
    x_im = _flip_backward(x_im)
    y = (jnp.einsum('dbsgp,dghp->bsgh', x_re, c_re)
         - jnp.einsum('dbsgp,dghp->bsgh', x_im, c_im)
         + d_skip * ug)
    y = jax.nn.gelu(y.reshape(bsz, seq, W_SSM))
    return y * jax.nn.sigmoid(y @ glu_w + glu_b)


def _rwkv7_mixer(xn, p_rkv, mu_rkv, mu_w, mu_a, mu_g, w0, w1, w2, a0, a1, a2,
                 g1, g2, k_k, k_a, r_k, gn_g, gn_b):
    bsz, seq, _ = xn.shape
    shifted = jnp.stack([_shift_prev(p_rkv), _shift_next(p_rkv)])
    rkv = p_rkv + (shifted - p_rkv) * mu_rkv[:, None, None, :]
    r, k, v = jnp.split(rkv, 3, axis=-1)
    x_prev = _shift_prev(xn)
    x_next = _shift_next(xn)
    dx = jnp.stack([x_prev, x_next]) - xn
    xw = xn + dx * mu_w[:, None, None, :]
    xa = xn + dx * mu_a[:, None, None, :]
    xg = xn + (0.5 * (x_prev + x_next) - xn) * mu_g
    w_lora = jnp.einsum('dbsr,dre->dbse', jnp.tanh(jnp.einsum('dbsc,dcr->dbsr', xw, w1)), w2)
    decay = jnp.exp(-RWKV_DECAY_SCALE * jax.nn.sigmoid(w0[:, None, None, :] + w_lora))
    a = jax.nn.sigmoid(a0[:, None, None, :]
                       + jnp.einsum('dbsr,dre->dbse', jnp.einsum('dbsc,dcr->dbsr', xa, a1), a2))
    g = jax.nn.sigmoid(xg @ g1) @ g2

    def heads(z):
        return z.reshape(2, bsz, seq, RWKV_HEADS, RWKV_HEAD)

    r, k, v, decay, a = heads(r), heads(k), heads(v), heads(decay), heads(a)
    kk = k * k_k.reshape(RWKV_HEADS, RWKV_HEAD)
    kkf = kk.astype(jnp.float32)
    kk = (kkf * lax.rsqrt(jnp.sum(kkf * kkf, -1, keepdims=True) + 1e-12)).astype(k.dtype)
    k = k * (1.0 + (a - 1.0) * k_a.reshape(RWKV_HEADS, RWKV_HEAD))
    bonus = jnp.sum(jnp.sum(r * k * r_k, -1, keepdims=True) * v, axis=0)

    def to_time(z):
        return jnp.moveaxis(_flip_backward(z), 2, 0)

    xs = (to_time(r), to_time(decay), to_time(k), to_time(v), to_time(-kk), to_time(kk * a))

    def step(state, inp):
        r_t, w_t, k_t, v_t, ka_t, kb_t = inp
        sa = jnp.einsum('dbhvk,dbhk->dbhv', state, ka_t)
        state = (state * w_t[..., None, :] + sa[..., :, None] * kb_t[..., None, :]
                 + v_t[..., :, None] * k_t[..., None, :])
        return state, jnp.einsum('dbhvk,dbhk->dbhv', state, r_t)

    state0 = jnp.zeros((2, bsz, RWKV_HEADS, RWKV_HEAD, RWKV_HEAD), r.dtype)
    _, ys = lax.scan(step, state0, xs)
    y = jnp.sum(_flip_backward(jnp.moveaxis(ys, 0, 2)), axis=0)
    yf = y.astype(jnp.float32)
    mu = jnp.mean(yf, -1, keepdims=True)
    var = jnp.mean(jnp.square(yf - mu), -1, keepdims=True)
    yn = ((yf - mu) * lax.rsqrt(var + RWKV_GN_EPS)).reshape(bsz, seq, W_RWKV)
    yn = (yn * gn_g + gn_b).astype(xn.dtype)
    return (yn + bonus.reshape(bsz, seq, W_RWKV)) * g


def _short_conv_mixer(p_conv, conv_w):
    bgate, cgate, xc = jnp.split(p_conv, 3, axis=-1)
    z = cgate * xc
    zc = conv_w[0] * _shift_prev(z) + conv_w[1] * z + conv_w[2] * _shift_next(z)
    return bgate * zc


def _fourier_mixer(p_fft):
    bsz, seq, _ = p_fft.shape
    z = p_fft.reshape(bsz, seq, FFT_GROUPS, FFT_CH).astype(jnp.float32)
    f = jnp.fft.fft2(z, axes=(1, 3), norm='ortho').real
    return f.reshape(bsz, seq, W_FFT).astype(p_fft.dtype)


def setup_inputs(seed: int = 0) -> dict:
    key = jax.random.key(seed)
    keys = iter(jax.random.split(key, 48))
    L = DEPTH
    G, P, H = SSM_GROUPS, SSM_STATE, SSM_CH

    def nrm(shape, scale):
        return scale * jax.random.normal(next(keys), shape, jnp.float32)

    def unif(shape, lo, hi):
        return jax.random.uniform(next(keys), shape, jnp.float32, lo, hi)

    inp = {}
    inp['x'] = nrm((BATCH, SEQ, D_MODEL), 1.0)
    inp['ln0_g'] = 1.0 + nrm((D_MODEL,), 0.02)
    inp['ln0_b'] = nrm((D_MODEL,), 0.02)
    inp['w_in'] = nrm((L, D_MODEL, D_IN_PROJ), D_MODEL ** -0.5)
    inp['s5_lambda_re'] = -0.5 + nrm((L, 2, G, P), 0.01)
    inp['s5_lambda_im'] = math.pi * jnp.arange(P, dtype=jnp.float32) + nrm((L, 2, G, P), 0.01)
    inp['s5_log_dt'] = unif((L, 2, G), math.log(DT_MIN), math.log(DT_MAX))
    inp['s5_b_re'] = nrm((L, G, P, H), (2 * H) ** -0.5)
    inp['s5_b_im'] = nrm((L, G, P, H), (2 * H) ** -0.5)
    inp['s5_c_re'] = nrm((L, 2, G, H, P), P ** -0.5)
    inp['s5_c_im'] = nrm((L, 2, G, H, P), P ** -0.5)
    inp['s5_d'] = nrm((L, G, H), 1.0)
    inp['s5_glu_w'] = nrm((L, W_SSM, W_SSM), W_SSM ** -0.5)
    inp['s5_glu_b'] = nrm((L, W_SSM), 0.02)
    inp['rwkv_mu_rkv'] = unif((L, 2, 3 * W_RWKV), 0.0, 1.0)
    inp['rwkv_mu_w'] = unif((L, 2, D_MODEL), 0.0, 1.0)
    inp['rwkv_mu_a'] = unif((L, 2, D_MODEL), 0.0, 1.0)
    inp['rwkv_mu_g'] = unif((L, D_MODEL), 0.0, 1.0)
    inp['rwkv_w0'] = nrm((L, 2, W_RWKV), 0.5)
    inp['rwkv_w1'] = nrm((L, 2, D_MODEL, DECAY_LORA), D_MODEL ** -0.5)
    inp['rwkv_w2'] = nrm((L, 2, DECAY_LORA, W_RWKV), DECAY_LORA ** -0.5)
    inp['rwkv_a0'] = nrm((L, 2, W_RWKV), 0.1)
    inp['rwkv_a1'] = nrm((L, 2, D_MODEL, AAA_LORA), D_MODEL ** -0.5)
    inp['rwkv_a2'] = nrm((L, 2, AAA_LORA, W_RWKV), AAA_LORA ** -0.5)
    inp['rwkv_g1'] = nrm((L, D_MODEL, GATE_LORA), D_MODEL ** -0.5)
    inp['rwkv_g2'] = nrm((L, GATE_LORA, W_RWKV), GATE_LORA ** -0.5)
    inp['rwkv_k_k'] = 0.85 + nrm((L, W_RWKV), 0.02)
    inp['rwkv_k_a'] = 1.0 + nrm((L, W_RWKV), 0.02)
    inp['rwkv_r_k'] = nrm((L, RWKV_HEADS, RWKV_HEAD), 0.1)
    inp['rwkv_gn_g'] = 1.0 + nrm((L, W_RWKV), 0.02)
    inp['rwkv_gn_b'] = nrm((L, W_RWKV), 0.02)
    inp['conv_w'] = nrm((L, CONV_WIDTH, W_CONV), CONV_WIDTH ** -0.5)
    inp['w_out'] = nrm((L, D_MIX, D_MODEL), DEEPNORM_BETA * D_MIX ** -0.5)
    inp['ln1_g'] = 1.0 + nrm((L, D_MODEL), 0.02)
    inp['ln1_b'] = nrm((L, D_MODEL), 0.02)
    inp['ffn_w1'] = nrm((L, D_MODEL, D_FF), D_MODEL ** -0.5)
    inp['ffn_w3'] = nrm((L, D_MODEL, D_FF), D_MODEL ** -0.5)
    inp['ffn_w2'] = nrm((L, D_FF, D_MODEL), DEEPNORM_BETA * D_FF ** -0.5)
    inp['ln2_g'] = 1.0 + nrm((L, D_MODEL), 0.02)
    inp['ln2_b'] = nrm((L, D_MODEL), 0.02)
    return inp


def reference(x, ln0_g, ln0_b, w_in,
              s5_lambda_re, s5_lambda_im, s5_log_dt, s5_b_re, s5_b_im, s5_c_re, s5_c_im,
              s5_d, s5_glu_w, s5_glu_b,
              rwkv_mu_rkv, rwkv_mu_w, rwkv_mu_a, rwkv_mu_g, rwkv_w0, rwkv_w1, rwkv_w2,
              rwkv_a0, rwkv_a1, rwkv_a2, rwkv_g1, rwkv_g2, rwkv_k_k, rwkv_k_a, rwkv_r_k,
              rwkv_gn_g, rwkv_gn_b,
              conv_w, w_out, ln1_g, ln1_b, ffn_w1, ffn_w3, ffn_w2, ln2_g, ln2_b):
    h = _layer_norm(x, ln0_g, ln0_b)
    splits = [W_SSM, W_SSM + 3 * W_RWKV, W_SSM + 3 * W_RWKV + 3 * W_CONV]
    for l in range(DEPTH):
        p = h @ w_in[l]
        p_ssm, p_rkv, p_conv, p_fft = jnp.split(p, splits, axis=-1)
        y_a = _s5_mixer(p_ssm, s5_lambda_re[l], s5_lambda_im[l], s5_log_dt[l],
                        s5_b_re[l], s5_b_im[l], s5_c_re[l], s5_c_im[l], s5_d[l],
                        s5_glu_w[l], s5_glu_b[l])
        y_b = _rwkv7_mixer(h, p_rkv, rwkv_mu_rkv[l], rwkv_mu_w[l], rwkv_mu_a[l], rwkv_mu_g[l],
                           rwkv_w0[l], rwkv_w1[l], rwkv_w2[l], rwkv_a0[l], rwkv_a1[l], rwkv_a2[l],
                           rwkv_g1[l], rwkv_g2[l], rwkv_k_k[l], rwkv_k_a[l], rwkv_r_k[l],
                           rwkv_gn_g[l], rwkv_gn_b[l])
        y_c = _short_conv_mixer(p_conv, conv_w[l])
        y_d = _fourier_mixer(p_fft)
        y = jnp.concatenate([y_a, y_b, y_c, y_d], axis=-1)
        h = _layer_norm(DEEPNORM_ALPHA * h + y @ w_out[l], ln1_g[l], ln1_b[l])
        f = (jax.nn.silu(h @ ffn_w1[l]) * (h @ ffn_w3[l])) @ ffn_w2[l]
        h = _layer_norm(DEEPNORM_ALPHA * h + f, ln2_g[l], ln2_b[l])
    return h
```

```python
import math

import contextlib
import numpy as np
import concourse.bass as bass
import concourse.mybir as mybir

F32 = mybir.dt.float32
BF16 = mybir.dt.bfloat16
AF = mybir.ActivationFunctionType
ALU = mybir.AluOpType
AX = mybir.AxisListType

NDSEM = {'sp': 8, 'pool': 6, 'act': 4}


class _Rec:
    def __init__(self):
        self.call = None

    def __getattr__(self, name):
        def f(*a, **k):
            self.call = (name, a, k)
            return self
        return f


class Prog:
    def __init__(self, nc, stack):
        self.nc = nc
        self.stack = stack
        self.eng_names = ['pe', 'act', 'dve', 'pool', 'sp']
        self.ops = {e: [] for e in self.eng_names}
        self.cnt = {e: 0 for e in self.eng_names}
        self.dcnt = {e: 0 for e in NDSEM}
        self.seen = {e: {} for e in self.eng_names}
        self.last_w = {}
        self.readers = {}
        self.sems = {}
        for e in ['pe', 'act', 'dve', 'pool']:
            self.sems['c_' + e] = stack.enter_context(nc.semaphore('c_' + e))
        for e, n in NDSEM.items():
            for i in range(n):
                self.sems['d_%s%d' % (e, i)] = stack.enter_context(nc.semaphore('d_%s%d' % (e, i)))
        self.ntile = 0
        self.pending = {e: {} for e in self.eng_names}
        self.dlast = {}
        self.psum_names = set()
        self.suffix = ''

    def sb(self, shape, dtype=F32, name=None):
        self.ntile += 1
        name = (name or 't%d' % self.ntile) + self.suffix
        return self.stack.enter_context(self.nc.sbuf_tensor(name, list(shape), dtype))

    def ps(self, shape, dtype=F32, name=None):
        self.ntile += 1
        name = name or 'p%d' % self.ntile
        self.psum_names.add(name)
        return self.stack.enter_context(self.nc.psum_tensor(name, list(shape), dtype))

    def _need(self, eng, tok, waits):
        sem, val = tok
        if eng == 'pe' and sem == 'c_pe':
            return
        if self.seen[eng].get(sem, 0) >= val:
            return
        if waits.get(sem, 0) < val:
            waits[sem] = val

    @staticmethod
    def _key(k):
        if isinstance(k, tuple):
            return (Prog._key(k[0]),) + tuple(k[1:])
        if isinstance(k, (str, int)):
            return k
        return k.name

    def barrier(self):
        toks = [('c_' + e, self.cnt[e]) for e in ['pe', 'act', 'dve', 'pool'] if self.cnt[e]] + list(self.dlast.items())
        for e in self.eng_names:
            for s_, v in toks:
                if self.pending[e].get(s_, 0) < v:
                    self.pending[e][s_] = v
        self.last_w = {}
        self.readers = {}

    def op(self, eng, fn, reads=(), writes=(), dma=False):
        waits = {}
        for s_, v in self.pending[eng].items():
            self._need(eng, (s_, v), waits)
        self.pending[eng] = {}
        reads = [self._key(k) for k in reads]
        writes = [self._key(k) for k in writes]
        def _isps(k):
            return (k[0] if isinstance(k, tuple) else k) in self.psum_names
        writes = writes + [k for k in reads if _isps(k) and k not in writes]
        reads = [k for k in reads if not _isps(k)]
        for k in reads:
            if k in self.last_w:
                self._need(eng, self.last_w[k], waits)
        for k in writes:
            if k in self.last_w:
                self._need(eng, self.last_w[k], waits)
            for r in self.readers.get(k, ()):
                self._need(eng, r, waits)
        if dma:
            i = self.dcnt[eng]
            self.dcnt[eng] += 1
            n = NDSEM[eng]
            sem = 'd_%s%d' % (eng, i % n)
            val = 16 * (i // n + 1)
            if i >= n:
                self._need(eng, (sem, val - 16), waits)
            tok = (sem, val)
            self.dlast[sem] = val
            inc = 16
        else:
            self.cnt[eng] += 1
            tok = ('c_' + eng, self.cnt[eng])
            inc = 1
        for s, v in waits.items():
            self.seen[eng][s] = v
        for k in writes:
            self.last_w[k] = tok
            self.readers[k] = []
        for k in reads:
            if k not in writes:
                self.readers.setdefault(k, []).append(tok)
        rec = _Rec()
        fn(rec)
        self.ops[eng].append((list(waits.items()), rec.call, tok, inc))
        return tok

    def dma(self, out, in_, reads=(), writes=(), q='sp', **kw):
        return self.op(q, lambda e: e.dma_start(out=out, in_=in_, **kw), reads, writes, dma=True)

    def flush(self):
        self.emit(final=False)

    def emit(self, final=True):
        nc = self.nc
        end_tokens = []
        for e, n in NDSEM.items():
            for i in range(min(n, self.dcnt[e])):
                cntv = (self.dcnt[e] - 1 - i) // n + 1
                end_tokens.append(('d_%s%d' % (e, i), 16 * cntv))
        for e in ['pe', 'act', 'dve', 'pool']:
            if self.cnt[e]:
                end_tokens.append(('c_' + e, self.cnt[e]))
        with nc.Block() as block:
            def run(eng_name):
                def body(eng):
                    for waits, fn, tok, inc in self.ops[eng_name]:
                        for s, v in waits:
                            eng.wait_ge(self.sems[s], v)
                        name, a, k = fn
                        ins = getattr(eng, name)(*a, **k)
                        ins.then_inc(self.sems[tok[0]], inc)
                    if eng_name == 'sp' and final:
                        for s, v in end_tokens:
                            eng.wait_ge(self.sems[s], v)
                    self.ops[eng_name] = []
                return body
            block.tensor(run('pe'))
            block.scalar(run('act'))
            block.vector(run('dve'))
            block.gpsimd(run('pool'))
            block.sync(run('sp'))

    def stats(self):
        return {e: len(self.ops[e]) for e in self.eng_names}


ALPHA = (2 * 2) ** 0.25
NT = 2048
NCH = 4


def consts_ones(P, val, name):
    t = P.sb([128, 128], F32, name)
    P.op('pool', lambda e: e.memset(t[:], val), [], [t])
    return t


def layernorm_fm(P, R, ncols_list, gcol, bcol, vecs, ones, ps, tmp, out_bf=None, eps=1e-5, nk=8, out_f32=True):
    for ci, (c0, n) in enumerate(ncols_list):
        pm, pq = ps[0], ps[1]
        for k in range(nk):
            sq = tmp['sq%d' % (k % 2)]
            P.op('act', lambda e, k=k, sq=sq: e.activation(out=sq[:, 0:n], in_=R[:, k, c0:c0 + n], func=AF.Square),
                 [(R, k, ci)], [sq])
            P.op('pe', lambda e, k=k: e.matmul(pm[:, 0:n], lhsT=ones[:], rhs=R[:, k, c0:c0 + n], start=(k == 0), stop=(k == nk - 1)),
                 [(R, k, ci), ones], [pm])
            P.op('pe', lambda e, k=k, sq=sq: e.matmul(pq[:, 0:n], lhsT=ones[:], rhs=sq[:, 0:n], start=(k == 0), stop=(k == nk - 1)),
                 [sq, ones], [pq])
        mean, msq, rstd = tmp['mean'], tmp['msq'], tmp['rstd']
        P.op('act', lambda e: e.activation(out=mean[:, 0:n], in_=pm[:, 0:n], func=AF.Copy), [pm], [mean])
        P.op('act', lambda e: e.activation(out=msq[:, 0:n], in_=pm[:, 0:n], func=AF.Square), [pm], [msq])
        P.op('dve', lambda e: e.tensor_tensor(out=rstd[:, 0:n], in0=pq[:, 0:n], in1=msq[:, 0:n], op=ALU.subtract), [pq, msq], [rstd])
        P.op('dve', lambda e: e.tensor_scalar(out=rstd[:, 0:n], in0=rstd[:, 0:n], scalar1=eps, scalar2=None, op0=ALU.add), [rstd], [rstd])
        P.op('act', lambda e: e.activation(out=rstd[:, 0:n], in_=rstd[:, 0:n], func=AF.Sqrt), [rstd], [rstd])
        P.op('dve', lambda e: e.reciprocal(out=rstd[:, 0:n], in_=rstd[:, 0:n]), [rstd], [rstd])
        for k in range(nk):
            t = tmp['t%d' % (k % 2)]
            P.op('dve', lambda e, k=k, t=t: e.tensor_tensor(out=t[:, 0:n], in0=R[:, k, c0:c0 + n], in1=mean[:, 0:n], op=ALU.subtract),
                 [(R, k, ci), mean], [t])
            P.op('dve', lambda e, t=t: e.tensor_tensor(out=t[:, 0:n], in0=t[:, 0:n], in1=rstd[:, 0:n], op=ALU.mult), [t, rstd], [t])
            if out_f32:
                P.op('act', lambda e, k=k, t=t: e.activation(out=R[:, k, c0:c0 + n], in_=t[:, 0:n], func=AF.Identity,
                                                              scale=vecs[:, gcol + k:gcol + k + 1], bias=vecs[:, bcol + k:bcol + k + 1]),
                     [t, vecs], [(R, k, ci)])
            if out_bf is not None:
                P.op('act', lambda e, k=k, t=t: e.activation(out=out_bf[:, k, c0:c0 + n], in_=t[:, 0:n], func=AF.Identity,
                                                              scale=vecs[:, gcol + k:gcol + k + 1], bias=vecs[:, bcol + k:bcol + k + 1]),
                     [t, vecs], [(out_bf, k, ci)])


V_GLUB, V_GNG, V_GNB, V_L1G, V_L1B, V_L2G, V_L2B = 0, 2, 4, 6, 14, 22, 30
NV = 38


def build_kc():
    nc = bass.Bass("TRN2", target_bir_lowering=False)
    dr = lambda n, s, kind="ExternalInput": nc.dram_tensor(n, s, F32, kind=kind).ap()
    hT = dr("hT", [1024, NT]); yin = dr("yin", [8, 256, NT]); vecs_d = dr("vecs", [128, NV])
    glu_w = dr("glu_w", [256, 256]); w_out = dr("w_out", [1024, 1024])
    fw1 = dr("fw1", [1024, 2816]); fw3 = dr("fw3", [1024, 2816]); fw2 = dr("fw2", [2816, 1024])
    outT = dr("outT", [1024, NT], "ExternalOutput")
    with contextlib.ExitStack() as st:
        P = Prog(nc, st)
        R = P.sb([128, 8, NT], F32, 'R')
        h1b = P.sb([128, 8, NT], BF16, 'h1b')
        G = P.sb([128, 11, NT], BF16, 'G')
        WB = P.sb([128, 11, 1024], BF16, 'WB')
        wst = [P.sb([128, 1024], F32, 'wst%d' % i) for i in range(2)]
        w13b = [P.sb([128, 8, 128], BF16, 'w13b%d' % i) for i in range(4)]
        vecs = P.sb([128, NV], F32, 'vecs_sb')
        gluw = P.sb([128, 2, 256], F32, 'gluw_sb')
        tmp = {n: P.sb([128, 512], F32, 'tmp_' + n) for n in
               ['sq0', 'sq1', 'mean', 'msq', 'rstd', 't0', 't1', 'a', 'b', 'c', 'd', 'e', 'f']}
        ps = [P.ps([128, 512], F32, 'ps%d' % i) for i in range(8)]
        ones = consts_ones(P, 1.0 / 1024, 'ones')
        bones = P.sb([128, 128], F32, 'bones')
        P.op('pool', lambda e: e.memset(bones[:], 0.0), [], [bones])
        P.op('pool', lambda e: e.memset(bones[0:64, 0:64], 1.0 / 64), [], [bones])
        P.op('pool', lambda e: e.memset(bones[64:128, 64:128], 1.0 / 64), [], [bones])

        P.dma(vecs[:], vecs_d, writes=[vecs])
        P.dma(gluw[:], glu_w.rearrange("(k p) j -> p k j", p=128), writes=[gluw])
        for k in range(8):
            P.dma(R[:, k, :], hT[k * 128:(k + 1) * 128, :], writes=[(R, k, c) for c in range(NCH)], q=('sp' if k % 2 == 0 else 'pool'))
        for k in range(8):
            P.dma(wst[k % 2][:], w_out[k * 128:(k + 1) * 128, :], writes=[wst[k % 2]])
            P.op('pool', lambda e, k=k: e.tensor_copy(out=WB[:, k, :], in_=wst[k % 2][:]), [wst[k % 2]], [(WB, k)])

        pi = [0]

        def nps():
            pi[0] = (pi[0] + 1) % 6
            return ps[2 + pi[0]]

        def ld(dst, idx, j, c):
            P.dma(dst[:], yin[idx, j * 128:(j + 1) * 128, c * 512:(c + 1) * 512], writes=[dst], q='pool')

        for c in range(NCH):
            cs = slice(c * 512, (c + 1) * 512)
            ya = []
            for j in range(2):
                y, u = tmp['a' if j == 0 else 'b'], tmp['c']
                ld(y, 0, j, c)
                P.op('act', lambda e, y=y, u=u: e.activation(out=u[:], in_=y[:], func=AF.Square), [y], [u])
                P.op('dve', lambda e, u=u: e.tensor_scalar(out=u[:], in0=u[:], scalar1=0.044715, scalar2=1.0, op0=ALU.mult, op1=ALU.add), [u], [u])
                P.op('dve', lambda e, y=y, u=u: e.tensor_tensor(out=u[:], in0=u[:], in1=y[:], op=ALU.mult), [u, y], [u])
                P.op('act', lambda e, u=u: e.activation(out=u[:], in_=u[:], func=AF.Sigmoid, scale=1.5957691216057308), [u], [u])
                P.op('dve', lambda e, y=y, u=u: e.tensor_tensor(out=y[:], in0=u[:], in1=y[:], op=ALU.mult), [u, y], [y])
                ya.append(y)
            for j in range(2):
                pp = nps()
                for k in range(2):
                    P.op('pe', lambda e, j=j, k=k, pp=pp: e.matmul(pp[:], lhsT=gluw[:, k, j * 128:(j + 1) * 128], rhs=ya[k][:], start=(k == 0), stop=(k == 1)),
                         [gluw, ya[k]], [pp])
                gt = tmp['d']
                P.op('act', lambda e, j=j, pp=pp, gt=gt: e.activation(out=gt[:], in_=pp[:], func=AF.Sigmoid, bias=vecs[:, V_GLUB + j:V_GLUB + j + 1]), [pp, vecs], [gt])
                P.op('dve', lambda e, j=j, gt=gt: e.tensor_tensor(out=G[:, j, cs], in0=gt[:], in1=ya[j][:], op=ALU.mult), [gt, ya[j]], [(G, j, c)])
            for j in range(2):
                yf, yb, sq = tmp['a'], tmp['b'], tmp['c']
                ld(yf, 1, j, c); ld(yb, 2, j, c)
                P.op('dve', lambda e: e.tensor_tensor(out=yf[:], in0=yf[:], in1=yb[:], op=ALU.add), [yf, yb], [yf])
                P.op('act', lambda e: e.activation(out=sq[:], in_=yf[:], func=AF.Square), [yf], [sq])
                pm, pq = nps(), nps()
                P.op('pe', lambda e, pm=pm: e.matmul(pm[:], lhsT=bones[:], rhs=yf[:], start=True, stop=True), [bones, yf], [pm])
                P.op('pe', lambda e, pq=pq: e.matmul(pq[:], lhsT=bones[:], rhs=sq[:], start=True, stop=True), [bones, sq], [pq])
                msq, rstd = tmp['msq'], tmp['rstd']
                P.op('act', lambda e, pm=pm: e.activation(out=msq[:], in_=pm[:], func=AF.Square), [pm], [msq])
                P.op('dve', lambda e, pq=pq: e.tensor_tensor(out=rstd[:], in0=pq[:], in1=msq[:], op=ALU.subtract), [pq, msq], [rstd])
                P.op('dve', lambda e: e.tensor_scalar(out=rstd[:], in0=rstd[:], scalar1=64e-5, scalar2=None, op0=ALU.add), [rstd], [rstd])
                P.op('act', lambda e: e.activation(out=rstd[:], in_=rstd[:], func=AF.Sqrt), [rstd], [rstd])
                P.op('dve', lambda e: e.reciprocal(out=rstd[:], in_=rstd[:]), [rstd], [rstd])
                P.op('dve', lambda e, pm=pm: e.tensor_tensor(out=yf[:], in0=yf[:], in1=pm[:], op=ALU.subtract), [yf, pm], [yf])
                P.op('dve', lambda e: e.tensor_tensor(out=yf[:], in0=yf[:], in1=rstd[:], op=ALU.mult), [yf, rstd], [yf])
                P.op('act', lambda e, j=j: e.activation(out=yf[:], in_=yf[:], func=AF.Identity, scale=vecs[:, V_GNG + j:V_GNG + j + 1],
                                                         bias=vecs[:, V_GNB + j:V_GNB + j + 1]), [yf, vecs], [yf])
                b1, b2, gg = tmp['d'], tmp['e'], tmp['f']
                ld(b1, 3, j, c); ld(b2, 4, j, c); ld(gg, 5, j, c)
                P.op('dve', lambda e: e.tensor_tensor(out=yf[:], in0=yf[:], in1=b1[:], op=ALU.add), [yf, b1], [yf])
                P.op('dve', lambda e: e.tensor_tensor(out=yf[:], in0=yf[:], in1=b2[:], op=ALU.add), [yf, b2], [yf])
                P.op('dve', lambda e, j=j: e.tensor_tensor(out=G[:, 2 + j, cs], in0=yf[:], in1=gg[:], op=ALU.mult), [yf, gg], [(G, 2 + j, c)])
            for idx, base in ((6, 4), (7, 6)):
                for j in range(2):
                    t = tmp['a' if j == 0 else 'b']
                    ld(t, idx, j, c)
                    slot = base + j
                    P.op('act', lambda e, t=t, slot=slot: e.activation(out=G[:, slot, cs], in_=t[:], func=AF.Copy), [t], [(G, slot, c)])
        for c in range(NCH):
            cs = slice(c * 512, (c + 1) * 512)
            for m in range(8):
                pp = nps()
                for k in range(8):
                    P.op('pe', lambda e, k=k, m=m, pp=pp: e.matmul(pp[:], lhsT=WB[:, k, m * 128:(m + 1) * 128], rhs=G[:, k, cs], start=(k == 0), stop=(k == 7)),
                         [(WB, k), (G, k, c)], [pp])
                P.op('dve', lambda e, m=m, pp=pp: e.scalar_tensor_tensor(out=R[:, m, cs], in0=R[:, m, cs], scalar=ALPHA, in1=pp[:], op0=ALU.mult, op1=ALU.add),
                     [(R, m, c), pp], [(R, m, c)])
        chunks = [(c * 512, 512) for c in range(NCH)]
        layernorm_fm(P, R, chunks, V_L1G, V_L1B, vecs, ones, ps, tmp, out_bf=h1b)
        for grp in range(2):
            for fl in range(11):
                ft = grp * 11 + fl
                wa, wb_ = w13b[(fl % 2) * 2], w13b[(fl % 2) * 2 + 1]
                for wi, (src, dst) in enumerate(((fw1, wa), (fw3, wb_))):
                    s = wst[wi]
                    P.dma(s[:].rearrange("p (k f) -> p k f", k=8), src[:, ft * 128:(ft + 1) * 128].rearrange("(k p) f -> p k f", p=128), writes=[s])
                    P.op('pool', lambda e, s=s, dst=dst: e.tensor_copy(out=dst[:], in_=s[:].rearrange("p (k f) -> p k f", k=8)), [s], [dst])
                for c in range(NCH):
                    cs = slice(c * 512, (c + 1) * 512)
                    pa, pb = nps(), nps()
                    for k in range(8):
                        P.op('pe', lambda e, k=k, pa=pa, wa=wa: e.matmul(pa[:], lhsT=wa[:, k, :], rhs=h1b[:, k, cs], start=(k == 0), stop=(k == 7)),
                             [wa, (h1b, k, c)], [pa])
                    for k in range(8):
                        P.op('pe', lambda e, k=k, pb=pb, wb_=wb_: e.matmul(pb[:], lhsT=wb_[:, k, :], rhs=h1b[:, k, cs], start=(k == 0), stop=(k == 7)),
                             [wb_, (h1b, k, c)], [pb])
                    sa = tmp['a' if c % 2 == 0 else 'b']
                    P.op('act', lambda e, pa=pa, sa=sa: e.activation(out=sa[:], in_=pa[:], func=AF.Silu), [pa], [sa])
                    P.op('dve', lambda e, pb=pb, sa=sa, fl=fl: e.tensor_tensor(out=G[:, fl, cs], in0=pb[:], in1=sa[:], op=ALU.mult), [pb, sa], [(G, fl, c)])
            for fl in range(11):
                ft = grp * 11 + fl
                s = wst[fl % 2]
                P.dma(s[:], fw2[ft * 128:(ft + 1) * 128, :], writes=[s])
                P.op('pool', lambda e, s=s, fl=fl: e.tensor_copy(out=WB[:, fl, :], in_=s[:]), [s], [(WB, fl)])
            for c in range(NCH):
                cs = slice(c * 512, (c + 1) * 512)
                for m in range(8):
                    pp = nps()
                    for fl in range(11):
                        P.op('pe', lambda e, fl=fl, m=m, pp=pp: e.matmul(pp[:], lhsT=WB[:, fl, m * 128:(m + 1) * 128], rhs=G[:, fl, cs], start=(fl == 0), stop=(fl == 10)),
                             [(WB, fl), (G, fl, c)], [pp])
                    if grp == 0:
                        P.op('dve', lambda e, m=m, pp=pp: e.scalar_tensor_tensor(out=R[:, m, cs], in0=R[:, m, cs], scalar=ALPHA, in1=pp[:], op0=ALU.mult, op1=ALU.add),
                             [(R, m, c), pp], [(R, m, c)])
                    else:
                        P.op('dve', lambda e, m=m, pp=pp: e.tensor_tensor(out=R[:, m, cs], in0=R[:, m, cs], in1=pp[:], op=ALU.add),
                             [(R, m, c), pp], [(R, m, c)])
        layernorm_fm(P, R, chunks, V_L2G, V_L2B, vecs, ones, ps, tmp)
        for k in range(8):
            P.dma(outT[k * 128:(k + 1) * 128, :], R[:, k, :], reads=[(R, k, c) for c in range(NCH)], q=('sp' if k % 2 == 0 else 'pool'))
        P.emit()
        print('K_C ops', P.stats())
    return nc


def kc_vecs(inp, l):
    v = np.zeros((128, NV), np.float32)
    def put(col, vec, n):
        v[:, col:col + n] = np.asarray(vec, np.float32).reshape(n, 128).T
    put(V_GLUB, inp['s5_glu_b'][l], 2); put(V_GNG, inp['rwkv_gn_g'][l], 2); put(V_GNB, inp['rwkv_gn_b'][l], 2)
    put(V_L1G, inp['ln1_g'][l], 8); put(V_L1B, inp['ln1_b'][l], 8); put(V_L2G, inp['ln2_g'][l], 8); put(V_L2B, inp['ln2_b'][l], 8)
    return v


NT = 2048
NX = NT + 2
C_FFT, AW, BW, AA, BA, AG, BG, NCOL = 1792, 2304, 2432, 2560, 2688, 2816, 2976, 3136
VA_L0G, VA_L0B, VA_MUW, VA_MUA, VA_MUG, VA_W0, VA_A0, VA_HM = 0, 8, 16, 32, 48, 56, 60, 64
VA_OMU = 66
NVA = 66 + 40


def build_ka(has_ln0):
    nc = bass.Bass("TRN2", target_bir_lowering=False)
    dr = lambda n, s, kind="ExternalInput": nc.dram_tensor(n, s, F32, kind=kind).ap()
    xT = dr("xT", [1024, NX]); vecs_d = dr("vecsA", [128, NVA]); w_in = dr("w_in", [1024, 2048])
    wfT_d = dr("wfT", [256, 1024]); d64_d = dr("d64", [256, 512])
    w1_d = dr("w1", [2, 1024, 64]); a1_d = dr("a1", [2, 1024, 64]); g1_d = dr("g1", [1024, 160])
    w2_d = dr("w2", [128, 256]); a2_d = dr("a2", [128, 256]); g2_d = dr("g2", [160, 256])
    pT = dr("pT", [2304, NT], "ExternalOutput"); lor = dr("lor", [1280, NT], "ExternalOutput")
    hTo = dr("hTo", [1024, NT], "ExternalOutput")
    with contextlib.ExitStack() as st:
        P = Prog(nc, st)
        X = P.sb([128, 8, NX], F32, 'X')
        hb = P.sb([128, 8, NX], BF16, 'hb')
        Wb = P.sb([128, 8, NCOL], BF16, 'Wb')
        ws = P.sb([128, 2048], F32, 'ws')
        w1s = P.sb([128, 2, 8, 64], F32, 'w1s'); a1s = P.sb([128, 2, 8, 64], F32, 'a1s'); g1s = P.sb([128, 8, 160], F32, 'g1s')
        wfT = P.sb([128, 2, 1024], F32, 'wfT_sb'); d64 = P.sb([128, 2, 512], F32, 'd64_sb')
        w2s = P.sb([128, 256], F32, 'w2s'); a2s = P.sb([128, 256], F32, 'a2s')
        g2s0 = P.sb([128, 256], F32, 'g2s0'); g2s1 = P.sb([32, 256], F32, 'g2s1')
        vecs = P.sb([128, NVA], F32, 'vecs_sb')
        tmp = {n: P.sb([128, 512], F32, 'tmp_' + n) for n in ['sq0', 'sq1', 'mean', 'msq', 'rstd', 't0', 't1']}
        ost = [P.sb([128, 512], F32, 'ost%d' % i) for i in range(4)]
        ps = [P.ps([128, 512], F32, 'ps%d' % i) for i in range(8)]
        ones = consts_ones(P, 1.0 / 1024, 'ones')

        P.dma(vecs[:], vecs_d, writes=[vecs])
        for k in range(8):
            P.dma(X[:, k, :], xT[k * 128:(k + 1) * 128, :], writes=[(X, k)], q=('sp' if k % 2 == 0 else 'pool'))
        P.dma(wfT[:], wfT_d.rearrange("(k p) d -> p k d", p=128), writes=[wfT])
        P.dma(d64[:], d64_d.rearrange("(k p) d -> p k d", p=128), writes=[d64])
        for d in range(2):
            P.dma(w1s[:, d], w1_d[d].rearrange("(k p) r -> p k r", p=128), writes=[w1s], q='pool')
            P.dma(a1s[:, d], a1_d[d].rearrange("(k p) r -> p k r", p=128), writes=[a1s], q='pool')
        P.dma(g1s[:], g1_d.rearrange("(k p) r -> p k r", p=128), writes=[g1s], q='pool')
        P.dma(w2s[:], w2_d, writes=[w2s]); P.dma(a2s[:], a2_d, writes=[a2s])
        P.dma(g2s0[:], g2_d[0:128, :], writes=[g2s0]); P.dma(g2s1[:], g2_d[128:160, :], writes=[g2s1])
        P.op('dve', lambda e: e.tensor_scalar(out=vecs[:, VA_OMU:VA_OMU + 40], in0=vecs[:, VA_MUW:VA_MUW + 40], scalar1=-1.0, scalar2=1.0,
                                              op0=ALU.mult, op1=ALU.add), [vecs], [vecs])
        chunks5 = [(0, 512), (512, 512), (1024, 512), (1536, 512), (2048, 2)]
        if has_ln0:
            layernorm_ka(P, X, chunks5, vecs, ones, ps, tmp, hb)
            for k in range(8):
                P.dma(hTo[k * 128:(k + 1) * 128, :], X[:, k, 1:1 + NT], reads=[(X, k)], q=('sp' if k % 2 == 0 else 'pool'))
        else:
            for k in range(8):
                P.op('act' if k % 2 else 'dve', (lambda e, k=k: e.activation(out=hb[:, k, :], in_=X[:, k, :], func=AF.Copy)) if k % 2 else
                     (lambda e, k=k: e.tensor_copy(out=hb[:, k, :], in_=X[:, k, :])), [(X, k)], [(hb, k)])
            P.dma(hTo[0:128, :], X[:, 0, 1:1 + NT], reads=[(X, 0)])
        P.op('dve', lambda e: e.tensor_scalar(out=hb[:, :, 0:1], in0=hb[:, :, 0:1], scalar1=vecs[:, VA_HM:VA_HM + 1], scalar2=None, op0=ALU.mult),
             [(hb, k) for k in range(8)] + [vecs], [(hb, k) for k in range(8)])
        P.op('dve', lambda e: e.tensor_scalar(out=hb[:, :, NX - 1:NX], in0=hb[:, :, NX - 1:NX], scalar1=vecs[:, VA_HM + 1:VA_HM + 2], scalar2=None, op0=ALU.mult),
             [(hb, k) for k in range(8)] + [vecs], [(hb, k) for k in range(8)])
        for k in range(8):
            P.dma(ws[:], w_in[k * 128:(k + 1) * 128, :], writes=[ws])
            P.op('pool', lambda e, k=k: e.tensor_copy(out=Wb[:, k, 0:C_FFT], in_=ws[:, 0:C_FFT]), [ws], [(Wb, k)])
        for m in range(8):
            pp = ps[m % 4]
            for kt in range(2):
                P.op('pe', lambda e, m=m, kt=kt, pp=pp: e.matmul(pp[:], lhsT=wfT[:, kt, m * 128:(m + 1) * 128], rhs=d64[:, kt, :], start=(kt == 0), stop=(kt == 1)),
                     [wfT, d64], [pp])
            P.op('act', lambda e, m=m, pp=pp: e.activation(out=Wb[:, m, C_FFT:AW], in_=pp[:], func=AF.Copy), [pp, (Wb, m)], [(Wb, m)])
        ei = 0
        for k in range(8):
            for (src, mu0, ca, cb) in ((w1s, VA_MUW, AW, BW), (a1s, VA_MUA, AA, BA)):
                for d in range(2):
                    for (col, vc) in ((ca, VA_OMU + (mu0 - VA_MUW)), (cb, mu0)):
                        eng = 'dve' if ei % 2 == 0 else 'pool'; ei += 1
                        P.op(eng, lambda e, k=k, d=d, src=src, col=col, vc=vc: e.tensor_scalar(
                            out=Wb[:, k, col + d * 64:col + d * 64 + 64], in0=src[:, d, k, :], scalar1=vecs[:, vc + d * 8 + k:vc + d * 8 + k + 1],
                            scalar2=None, op0=ALU.mult), [src, vecs, (Wb, k)], [(Wb, k)])
            for (col, vc) in ((AG, VA_OMU + (VA_MUG - VA_MUW)), (BG, VA_MUG)):
                eng = 'dve' if ei % 2 == 0 else 'pool'; ei += 1
                P.op(eng, lambda e, k=k, col=col, vc=vc: e.tensor_scalar(out=Wb[:, k, col:col + 160], in0=g1s[:, k, :], scalar1=vecs[:, vc + k:vc + k + 1],
                                                                        scalar2=None, op0=ALU.mult), [g1s, vecs, (Wb, k)], [(Wb, k)])
        pi = [0]

        def nps():
            pi[0] = (pi[0] + 1) % 8
            return ps[pi[0]]
        wkeys = [(Wb, k) for k in range(8)]
        oi = 0
        for mt in range(18):
            for c in range(4):
                pp = nps()
                for k in range(8):
                    P.op('pe', lambda e, k=k, mt=mt, c=c, pp=pp: e.matmul(pp[:], lhsT=Wb[:, k, mt * 128:(mt + 1) * 128], rhs=hb[:, k, 1 + c * 512:1 + (c + 1) * 512],
                                                                        start=(k == 0), stop=(k == 7)), [(Wb, k), (hb, k)], [pp])
                o = ost[oi % 4]; oi += 1
                if oi % 2:
                    P.op('act', lambda e, o=o, pp=pp: e.activation(out=o[:], in_=pp[:], func=AF.Copy), [pp], [o])
                else:
                    P.op('dve', lambda e, o=o, pp=pp: e.tensor_copy(out=o[:], in_=pp[:]), [pp], [o])
                P.dma(pT[mt * 128:(mt + 1) * 128, c * 512:(c + 1) * 512], o[:], reads=[o], q=('sp' if oi % 2 else 'pool'))
        ltiles = [(AW, 128, 0, False), (BW, 128, 1, True), (AA, 128, 2, False), (BA, 128, 3, True),
                  (AG, 128, 4, False), (AG + 128, 32, 5, False), (BG, 128, 6, True), (BG + 128, 32, 7, True)]
        for (col, wd, j, halo) in ltiles:
            for c in range(4):
                pp = nps()
                for k in range(8):
                    P.op('pe', lambda e, k=k, c=c, pp=pp, col=col, wd=wd: e.matmul(pp[0:wd, :], lhsT=Wb[:, k, col:col + wd], rhs=hb[:, k, 1 + c * 512:1 + (c + 1) * 512],
                                                                                  start=(k == 0), stop=(k == 7)), [(Wb, k), (hb, k)], [pp])
                P.op('act' if c % 2 else 'dve', (lambda e, c=c, pp=pp, j=j, wd=wd: e.activation(out=X[0:wd, j, 1 + c * 512:1 + (c + 1) * 512], in_=pp[0:wd, :], func=AF.Copy)) if c % 2 else
                     (lambda e, c=c, pp=pp, j=j, wd=wd: e.tensor_copy(out=X[0:wd, j, 1 + c * 512:1 + (c + 1) * 512], in_=pp[0:wd, :])), [pp, (X, j)], [(X, j)])
            if halo:
                pp = nps()
                for k in range(8):
                    P.op('pe', lambda e, k=k, pp=pp, col=col, wd=wd: e.matmul(pp[0:wd, 0:2], lhsT=Wb[:, k, col:col + wd], rhs=hb[:, k, 0:NX:NX - 1],
                                                                             start=(k == 0), stop=(k == 7)), [(Wb, k), (hb, k)], [pp])
                P.op('dve', lambda e, pp=pp, j=j, wd=wd: e.tensor_copy(out=X[0:wd, j, 0:NX:NX - 1], in_=pp[0:wd, 0:2]), [pp, (X, j)], [(X, j)])
        L = lambda j, r0, r1, a, b: X[r0:r1, j, a:b]
        for c in range(4):
            a0, b0 = 1 + c * 512, 1 + (c + 1) * 512
            for (ja, jb, w2t, bcol, obase, do_tanh) in ((0, 1, w2s, VA_W0, 0, True), (2, 3, a2s, VA_A0, 512, False)):
                H = tmp['t0']
                P.op('dve', lambda e, ja=ja, jb=jb, H=H: e.tensor_tensor(out=H[0:64, :], in0=L(ja, 0, 64, a0, b0), in1=L(jb, 0, 64, a0 - 1, b0 - 1), op=ALU.add),
                     [(X, ja), (X, jb)], [H])
                P.op('dve', lambda e, ja=ja, jb=jb, H=H: e.tensor_tensor(out=H[64:128, :], in0=L(ja, 64, 128, a0, b0), in1=L(jb, 64, 128, a0 + 1, b0 + 1), op=ALU.add),
                     [(X, ja), (X, jb)], [H])
                if do_tanh:
                    P.op('act', lambda e, H=H: e.activation(out=H[:], in_=H[:], func=AF.Tanh), [H], [H])
                for d in range(2):
                    for mt in range(2):
                        pp = nps()
                        P.op('pe', lambda e, d=d, mt=mt, pp=pp, H=H, w2t=w2t: e.matmul(pp[:], lhsT=w2t[d * 64:(d + 1) * 64, mt * 128:(mt + 1) * 128], rhs=H[d * 64:(d + 1) * 64, :],
                                                                                      start=True, stop=True), [w2t, H], [pp])
                        o = ost[oi % 4]; oi += 1
                        P.op('act', lambda e, o=o, pp=pp, d=d, mt=mt, bcol=bcol: e.activation(out=o[:], in_=pp[:], func=AF.Sigmoid, bias=vecs[:, bcol + d * 2 + mt:bcol + d * 2 + mt + 1]),
                             [pp, vecs], [o])
                        r0 = obase + d * 256 + mt * 128
                        P.dma(lor[r0:r0 + 128, c * 512:(c + 1) * 512], o[:], reads=[o], q=('sp' if oi % 2 else 'pool'))
            Hg = [tmp['t1'], tmp['sq0']]
            for (ja, jb, H, wd) in ((4, 6, Hg[0], 128), (5, 7, Hg[1], 32)):
                P.op('dve', lambda e, jb=jb, H=H, wd=wd: e.tensor_tensor(out=H[0:wd, :], in0=L(jb, 0, wd, a0 - 1, b0 - 1), in1=L(jb, 0, wd, a0 + 1, b0 + 1), op=ALU.add),
                     [(X, jb)], [H])
                P.op('dve', lambda e, ja=ja, H=H, wd=wd: e.scalar_tensor_tensor(out=H[0:wd, :], in0=H[0:wd, :], scalar=0.5, in1=L(ja, 0, wd, a0, b0), op0=ALU.mult, op1=ALU.add),
                     [(X, ja), H], [H])
                P.op('act', lambda e, H=H, wd=wd: e.activation(out=H[0:wd, :], in_=H[0:wd, :], func=AF.Sigmoid), [H], [H])
            for mt in range(2):
                pp = nps()
                P.op('pe', lambda e, mt=mt, pp=pp: e.matmul(pp[:], lhsT=g2s0[:, mt * 128:(mt + 1) * 128], rhs=Hg[0][:], start=True, stop=False), [g2s0, Hg[0]], [pp])
                P.op('pe', lambda e, mt=mt, pp=pp: e.matmul(pp[:], lhsT=g2s1[:, mt * 128:(mt + 1) * 128], rhs=Hg[1][0:32, :], start=False, stop=True), [g2s1, Hg[1]], [pp])
                o = ost[oi % 4]; oi += 1
                P.op('dve', lambda e, o=o, pp=pp: e.tensor_copy(out=o[:], in_=pp[:]), [pp], [o])
                r0 = 1024 + mt * 128
                P.dma(lor[r0:r0 + 128, c * 512:(c + 1) * 512], o[:], reads=[o], q=('sp' if oi % 2 else 'pool'))
        P.emit()
        print('K_A ops', P.stats())
    return nc


def layernorm_ka(P, X, chunks, vecs, ones, ps, tmp, hb, eps=1e-5):
    for ci, (c0, n) in enumerate(chunks):
        pm, pq = ps[0], ps[1]
        for k in range(8):
            sq = tmp['sq%d' % (k % 2)]
            P.op('act', lambda e, k=k, sq=sq: e.activation(out=sq[:, 0:n], in_=X[:, k, c0:c0 + n], func=AF.Square), [(X, k)], [sq])
            P.op('pe', lambda e, k=k: e.matmul(pm[:, 0:n], lhsT=ones[:], rhs=X[:, k, c0:c0 + n], start=(k == 0), stop=(k == 7)), [(X, k), ones], [pm])
            P.op('pe', lambda e, k=k, sq=sq: e.matmul(pq[:, 0:n], lhsT=ones[:], rhs=sq[:, 0:n], start=(k == 0), stop=(k == 7)), [sq, ones], [pq])
        mean, msq, rstd = tmp['mean'], tmp['msq'], tmp['rstd']
        P.op('act', lambda e: e.activation(out=mean[:, 0:n], in_=pm[:, 0:n], func=AF.Copy), [pm], [mean])
        P.op('act', lambda e: e.activation(out=msq[:, 0:n], in_=pm[:, 0:n], func=AF.Square), [pm], [msq])
        P.op('dve', lambda e: e.tensor_tensor(out=rstd[:, 0:n], in0=pq[:, 0:n], in1=msq[:, 0:n], op=ALU.subtract), [pq, msq], [rstd])
        P.op('dve', lambda e: e.tensor_scalar(out=rstd[:, 0:n], in0=rstd[:, 0:n], scalar1=eps, scalar2=None, op0=ALU.add), [rstd], [rstd])
        P.op('act', lambda e: e.activation(out=rstd[:, 0:n], in_=rstd[:, 0:n], func=AF.Sqrt), [rstd], [rstd])
        P.op('dve', lambda e: e.reciprocal(out=rstd[:, 0:n], in_=rstd[:, 0:n]), [rstd], [rstd])
        for k in range(8):
            t = tmp['t%d' % (k % 2)]
            P.op('dve', lambda e, k=k, t=t: e.tensor_tensor(out=t[:, 0:n], in0=X[:, k, c0:c0 + n], in1=mean[:, 0:n], op=ALU.subtract), [(X, k), mean], [t])
            P.op('dve', lambda e, t=t: e.tensor_tensor(out=t[:, 0:n], in0=t[:, 0:n], in1=rstd[:, 0:n], op=ALU.mult), [t, rstd], [t])
            P.op('act', lambda e, k=k, t=t: e.activation(out=X[:, k, c0:c0 + n], in_=t[:, 0:n], func=AF.Identity, scale=vecs[:, VA_L0G + k:VA_L0G + k + 1],
                                                          bias=vecs[:, VA_L0B + k:VA_L0B + k + 1]), [t, vecs], [(X, k)])
            P.op('act', lambda e, k=k, t=t: e.activation(out=hb[:, k, c0:c0 + n], in_=t[:, 0:n], func=AF.Identity, scale=vecs[:, VA_L0G + k:VA_L0G + k + 1],
                                                          bias=vecs[:, VA_L0B + k:VA_L0B + k + 1]), [t, vecs], [(hb, k)])


def ka_consts():
    c = np.arange(64)
    ang = 2 * np.pi * np.outer(c, c) / 64.0
    d = np.zeros((256, 512), np.float64)
    for g in range(4):
        d[g * 64:(g + 1) * 64, g * 64:(g + 1) * 64] = np.cos(ang) / 8.0
        d[g * 64:(g + 1) * 64, 256 + g * 64:256 + (g + 1) * 64] = -np.sin(ang) / 8.0
    return d.astype(np.float32)


def ka_vecs(inp, l, hm):
    v = np.zeros((128, NVA), np.float32)
    def put(col, vec, n):
        v[:, col:col + n] = np.asarray(vec, np.float32).reshape(n, 128).T
    put(VA_L0G, inp['ln0_g'], 8); put(VA_L0B, inp['ln0_b'], 8)
    for d in range(2):
        put(VA_MUW + d * 8, inp['rwkv_mu_w'][l, d], 8); put(VA_MUA + d * 8, inp['rwkv_mu_a'][l, d], 8)
        put(VA_W0 + d * 2, inp['rwkv_w0'][l, d], 2); put(VA_A0 + d * 2, inp['rwkv_a0'][l, d], 2)
    put(VA_MUG, inp['rwkv_mu_g'][l], 8)
    v[:, VA_HM] = hm[0]; v[:, VA_HM + 1] = hm[1]
    return v


def ka_inputs(inp, l, xT_halo, hm):
    return {"xT": xT_halo, "vecsA": ka_vecs(inp, l, hm), "w_in": inp['w_in'][l],
            "wfT": np.ascontiguousarray(inp['w_in'][l][:, 1792:2048].T), "d64": ka_consts(),
            "w1": inp['rwkv_w1'][l], "a1": inp['rwkv_a1'][l], "g1": inp['rwkv_g1'][l],
            "w2": np.ascontiguousarray(inp['rwkv_w2'][l].reshape(128, 256)), "a2": np.ascontiguousarray(inp['rwkv_a2'][l].reshape(128, 256)),
            "g2": inp['rwkv_g2'][l]}


SEQ = 8192


def ka_phase(P, D, ps, has_ln0):
    xT_full, vecs_d, w_in = D['hfull'], D['vecsA'], D['w_in']
    X = P.sb([128, 8, NX], F32, 'X')
    hb = P.sb([128, 8, NX], BF16, 'hb')
    Wb = P.sb([128, 8, NCOL], BF16, 'Wb')
    ws = P.sb([128, 2048], F32, 'ws')
    w1s = P.sb([128, 2, 8, 64], F32, 'w1s'); a1s = P.sb([128, 2, 8, 64], F32, 'a1s'); g1s = P.sb([128, 8, 160], F32, 'g1s')
    wfT = P.sb([128, 2, 1024], F32, 'wfT_sb'); d64 = P.sb([128, 2, 512], F32, 'd64_sb')
    w2s = P.sb([128, 256], F32, 'w2s'); a2s = P.sb([128, 256], F32, 'a2s')
    g2s0 = P.sb([128, 256], F32, 'g2s0'); g2s1 = P.sb([32, 256], F32, 'g2s1')
    vecs = P.sb([128, NVA], F32, 'vecs_sb')
    tmp = {n: P.sb([128, 512], F32, 'tmp_' + n) for n in ['sq0', 'sq1', 'mean', 'msq', 'rstd', 't0', 't1']}
    ost = [P.sb([128, 512], F32, 'ost%d' % i) for i in range(4)]
    ones = consts_ones(P, 1.0 / 1024, 'ones')

    P.dma(vecs[:], vecs_d, writes=[vecs])
    P.dma(wfT[:], D['wfT'].rearrange("(k p) d -> p k d", p=128), writes=[wfT])
    P.dma(d64[:], D['d64'].rearrange("(k p) d -> p k d", p=128), writes=[d64])
    for d in range(2):
        P.dma(w1s[:, d], D['w1'][d].rearrange("(k p) r -> p k r", p=128), writes=[w1s], q='pool')
        P.dma(a1s[:, d], D['a1'][d].rearrange("(k p) r -> p k r", p=128), writes=[a1s], q='pool')
    P.dma(g1s[:], D['g1'].rearrange("(k p) r -> p k r", p=128), writes=[g1s], q='pool')
    P.dma(w2s[:], D['w2'], writes=[w2s]); P.dma(a2s[:], D['a2'], writes=[a2s])
    P.dma(g2s0[:], D['g2'][0:128, :], writes=[g2s0]); P.dma(g2s1[:], D['g2'][128:160, :], writes=[g2s1])
    P.op('dve', lambda e: e.tensor_scalar(out=vecs[:, VA_OMU:VA_OMU + 40], in0=vecs[:, VA_MUW:VA_MUW + 40], scalar1=-1.0, scalar2=1.0,
                                          op0=ALU.mult, op1=ALU.add), [vecs], [vecs])
    for k in range(8):
        P.dma(ws[:], w_in[k * 128:(k + 1) * 128, :], writes=[ws])
        P.op('pool', lambda e, k=k: e.tensor_copy(out=Wb[:, k, 0:C_FFT], in_=ws[:, 0:C_FFT]), [ws], [(Wb, k)])
    for m in range(8):
        pp = ps[m % 4]
        for kt in range(2):
            P.op('pe', lambda e: e.matmul(pp[:], lhsT=wfT[:, kt, m * 128:(m + 1) * 128], rhs=d64[:, kt, :], start=(kt == 0), stop=(kt == 1)), [wfT, d64], [pp])
        P.op('act', lambda e: e.activation(out=Wb[:, m, C_FFT:AW], in_=pp[:], func=AF.Copy), [pp, (Wb, m)], [(Wb, m)])
    ei = 0
    for k in range(8):
        for (src, mu0, ca, cb) in ((w1s, VA_MUW, AW, BW), (a1s, VA_MUA, AA, BA)):
            for d in range(2):
                for (col, vc) in ((ca, VA_OMU + (mu0 - VA_MUW)), (cb, mu0)):
                    eng = 'dve' if ei % 2 == 0 else 'pool'; ei += 1
                    P.op(eng, lambda e: e.tensor_scalar(out=Wb[:, k, col + d * 64:col + d * 64 + 64], in0=src[:, d, k, :],
                                                        scalar1=vecs[:, vc + d * 8 + k:vc + d * 8 + k + 1], scalar2=None, op0=ALU.mult), [src, vecs, (Wb, k)], [(Wb, k)])
        for (col, vc) in ((AG, VA_OMU + (VA_MUG - VA_MUW)), (BG, VA_MUG)):
            eng = 'dve' if ei % 2 == 0 else 'pool'; ei += 1
            P.op(eng, lambda e: e.tensor_scalar(out=Wb[:, k, col:col + 160], in0=g1s[:, k, :], scalar1=vecs[:, vc + k:vc + k + 1],
                                                scalar2=None, op0=ALU.mult), [g1s, vecs, (Wb, k)], [(Wb, k)])
    pi = [0]

    def nps():
        pi[0] = (pi[0] + 1) % 8
        return ps[pi[0]]
    oi = 0
    chunks5 = [(0, 512), (512, 512), (1024, 512), (1536, 512), (2048, 2)]
    for tq in range(4):
        t0 = tq * NT
        lo = t0 - 1
        c_lo = 1 if tq == 0 else 0
        c_hi = NX - 1 if tq == 3 else NX
        for k in range(8):
            P.dma(X[:, k, c_lo:c_hi], xT_full[k * 128:(k + 1) * 128, lo + c_lo:lo + c_hi], writes=[(X, k)], q=('sp' if k % 2 == 0 else 'pool'))
            if tq == 0:
                P.op('pool', lambda e: e.memset(X[:, k, 0:1], 0.0), [(X, k)], [(X, k)])
            if tq == 3:
                P.op('pool', lambda e: e.memset(X[:, k, NX - 1:NX], 0.0), [(X, k)], [(X, k)])
        if has_ln0:
            layernorm_ka(P, X, chunks5, vecs, ones, ps, tmp, hb)
            for k in range(8):
                P.dma(D['h0'][k * 128:(k + 1) * 128, t0:t0 + NT], X[:, k, 1:1 + NT], reads=[(X, k)], q=('sp' if k % 2 == 0 else 'pool'))
        else:
            for k in range(8):
                if k % 2:
                    P.op('act', lambda e: e.activation(out=hb[:, k, :], in_=X[:, k, :], func=AF.Copy), [(X, k)], [(hb, k)])
                else:
                    P.op('dve', lambda e: e.tensor_copy(out=hb[:, k, :], in_=X[:, k, :]), [(X, k)], [(hb, k)])
        hbk = [(hb, k) for k in range(8)]
        if tq == 0:
            P.op('pool', lambda e: e.memset(hb[:, :, 0:1], 0.0), hbk, hbk)
        if tq == 3:
            P.op('pool', lambda e: e.memset(hb[:, :, NX - 1:NX], 0.0), hbk, hbk)
        for mt in range(18):
            for c in range(4):
                pp = nps()
                for k in range(8):
                    P.op('pe', lambda e: e.matmul(pp[:], lhsT=Wb[:, k, mt * 128:(mt + 1) * 128], rhs=hb[:, k, 1 + c * 512:1 + (c + 1) * 512],
                                                  start=(k == 0), stop=(k == 7)), [(Wb, k), (hb, k)], [pp])
                o = ost[oi % 4]; oi += 1
                if oi % 2:
                    P.op('act', lambda e: e.activation(out=o[:], in_=pp[:], func=AF.Copy), [pp], [o])
                else:
                    P.op('dve', lambda e: e.tensor_copy(out=o[:], in_=pp[:]), [pp], [o])
                P.dma(D['pT'][mt * 128:(mt + 1) * 128, t0 + c * 512:t0 + (c + 1) * 512], o[:], reads=[o], q=('sp' if oi % 2 else 'pool'))
        ltiles = [(AW, 128, 0, False), (BW, 128, 1, True), (AA, 128, 2, False), (BA, 128, 3, True),
                  (AG, 128, 4, False), (AG + 128, 32, 5, False), (BG, 128, 6, True), (BG + 128, 32, 7, True)]
        for (col, wd, j, halo) in ltiles:
            for c in range(4):
                pp = nps()
                for k in range(8):
                    P.op('pe', lambda e: e.matmul(pp[0:wd, :], lhsT=Wb[:, k, col:col + wd], rhs=hb[:, k, 1 + c * 512:1 + (c + 1) * 512],
                                                  start=(k == 0), stop=(k == 7)), [(Wb, k), (hb, k)], [pp])
                if c % 2:
                    P.op('act', lambda e: e.activation(out=X[0:wd, j, 1 + c * 512:1 + (c + 1) * 512], in_=pp[0:wd, :], func=AF.Copy), [pp, (X, j)], [(X, j)])
                else:
                    P.op('dve', lambda e: e.tensor_copy(out=X[0:wd, j, 1 + c * 512:1 + (c + 1) * 512], in_=pp[0:wd, :]), [pp, (X, j)], [(X, j)])
            if halo:
                pp = nps()
                for k in range(8):
                    P.op('pe', lambda e: e.matmul(pp[0:wd, 0:2], lhsT=Wb[:, k, col:col + wd], rhs=hb[:, k, 0:NX:NX - 1],
                                                  start=(k == 0), stop=(k == 7)), [(Wb, k), (hb, k)], [pp])
                P.op('dve', lambda e: e.tensor_copy(out=X[0:wd, j, 0:NX:NX - 1], in_=pp[0:wd, 0:2]), [pp, (X, j)], [(X, j)])
        Lx = lambda j, r0, r1, a, b: X[r0:r1, j, a:b]
        for c in range(4):
            a0, b0 = 1 + c * 512, 1 + (c + 1) * 512
            for (ja, jb, w2t, bcol, obase, do_tanh) in ((0, 1, w2s, VA_W0, 0, True), (2, 3, a2s, VA_A0, 512, False)):
                H = tmp['t0']
                P.op('dve', lambda e: e.tensor_tensor(out=H[0:64, :], in0=Lx(ja, 0, 64, a0, b0), in1=Lx(jb, 0, 64, a0 - 1, b0 - 1), op=ALU.add), [(X, ja), (X, jb)], [H])
                P.op('dve', lambda e: e.tensor_tensor(out=H[64:128, :], in0=Lx(ja, 64, 128, a0, b0), in1=Lx(jb, 64, 128, a0 + 1, b0 + 1), op=ALU.add), [(X, ja), (X, jb), H], [H])
                if do_tanh:
                    P.op('act', lambda e: e.activation(out=H[:], in_=H[:], func=AF.Tanh), [H], [H])
                for d in range(2):
                    for mt in range(2):
                        pp = nps()
                        P.op('pe', lambda e: e.matmul(pp[:], lhsT=w2t[d * 64:(d + 1) * 64, mt * 128:(mt + 1) * 128], rhs=H[d * 64:(d + 1) * 64, :], start=True, stop=True), [w2t, H], [pp])
                        o = ost[oi % 4]; oi += 1
                        P.op('act', lambda e: e.activation(out=o[:], in_=pp[:], func=AF.Sigmoid, bias=vecs[:, bcol + d * 2 + mt:bcol + d * 2 + mt + 1]), [pp, vecs], [o])
                        r0 = obase + d * 256 + mt * 128
                        P.dma(D['lor'][r0:r0 + 128, t0 + c * 512:t0 + (c + 1) * 512], o[:], reads=[o], q=('sp' if oi % 2 else 'pool'))
            Hg = [tmp['t1'], tmp['sq0']]
            for (ja, jb, H, wd) in ((4, 6, Hg[0], 128), (5, 7, Hg[1], 32)):
                P.op('dve', lambda e: e.tensor_tensor(out=H[0:wd, :], in0=Lx(jb, 0, wd, a0 - 1, b0 - 1), in1=Lx(jb, 0, wd, a0 + 1, b0 + 1), op=ALU.add), [(X, jb)], [H])
                P.op('dve', lambda e: e.scalar_tensor_tensor(out=H[0:wd, :], in0=H[0:wd, :], scalar=0.5, in1=Lx(ja, 0, wd, a0, b0), op0=ALU.mult, op1=ALU.add), [(X, ja), H], [H])
                P.op('act', lambda e: e.activation(out=H[0:wd, :], in_=H[0:wd, :], func=AF.Sigmoid), [H], [H])
            for mt in range(2):
                pp = nps()
                P.op('pe', lambda e: e.matmul(pp[:], lhsT=g2s0[:, mt * 128:(mt + 1) * 128], rhs=Hg[0][:], start=True, stop=False), [g2s0, Hg[0]], [pp])
                P.op('pe', lambda e: e.matmul(pp[:], lhsT=g2s1[:, mt * 128:(mt + 1) * 128], rhs=Hg[1][0:32, :], start=False, stop=True), [g2s1, Hg[1]], [pp])
                o = ost[oi % 4]; oi += 1
                P.op('dve', lambda e: e.tensor_copy(out=o[:], in_=pp[:]), [pp], [o])
                r0 = 1024 + mt * 128
                P.dma(D['lor'][r0:r0 + 128, t0 + c * 512:t0 + (c + 1) * 512], o[:], reads=[o], q=('sp' if oi % 2 else 'pool'))


def kc_phase(P, D, ps):
    R = P.sb([128, 8, NT], F32, 'R')
    h1b = P.sb([128, 8, NT], BF16, 'h1b')
    G = P.sb([128, 11, NT], BF16, 'G')
    WB = P.sb([128, 11, 1024], BF16, 'WB')
    wst = [P.sb([128, 1024], F32, 'wst%d' % i) for i in range(2)]
    w13b = [P.sb([128, 8, 128], BF16, 'w13b%d' % i) for i in range(4)]
    vecs = P.sb([128, NV], F32, 'vecs_sb')
    gluw = P.sb([128, 2, 256], F32, 'gluw_sb')
    tmp = {n: P.sb([128, 512], F32, 'tmp_' + n) for n in
           ['sq0', 'sq1', 'mean', 'msq', 'rstd', 't0', 't1', 'a', 'b', 'c', 'd', 'e', 'f']}
    ones = consts_ones(P, 1.0 / 1024, 'ones')
    bones = P.sb([128, 128], F32, 'bones')
    P.op('pool', lambda e: e.memset(bones[:], 0.0), [], [bones])
    P.op('pool', lambda e: e.memset(bones[0:64, 0:64], 1.0 / 64), [], [bones])
    P.op('pool', lambda e: e.memset(bones[64:128, 64:128], 1.0 / 64), [], [bones])

    P.dma(vecs[:], D['vecs'], writes=[vecs])
    P.dma(gluw[:], D['glu_w'].rearrange("(k p) j -> p k j", p=128), writes=[gluw])

    for tq in range(4):
        t0 = tq * NT
        for k in range(8):
            P.dma(R[:, k, :], D['hT'][k * 128:(k + 1) * 128, t0:t0 + NT], writes=[(R, k, c) for c in range(NCH)], q=('sp' if k % 2 == 0 else 'pool'))
        for k in range(8):
            P.dma(wst[k % 2][:], D['w_out'][k * 128:(k + 1) * 128, :], writes=[wst[k % 2]])
            P.op('pool', lambda e, k=k: e.tensor_copy(out=WB[:, k, :], in_=wst[k % 2][:]), [wst[k % 2]], [(WB, k)])

        pi = [0]

        def nps():
            pi[0] = (pi[0] + 1) % 6
            return ps[2 + pi[0]]

        def ld(dst, idx, j, c):
            if idx == 5:
                src = D['lor'][1024 + j * 128:1024 + (j + 1) * 128, t0 + c * 512:t0 + (c + 1) * 512]
            else:
                src = D['yin'][idx * 256 + j * 128:idx * 256 + (j + 1) * 128, t0 + c * 512:t0 + (c + 1) * 512]
            P.dma(dst[:], src, writes=[dst], q='pool')

        for c in range(NCH):
            cs = slice(c * 512, (c + 1) * 512)
            ya = []
            for j in range(2):
                y, u = tmp['a' if j == 0 else 'b'], tmp['c']
                ld(y, 0, j, c)
                P.op('act', lambda e, y=y, u=u: e.activation(out=u[:], in_=y[:], func=AF.Square), [y], [u])
                P.op('dve', lambda e, u=u: e.tensor_scalar(out=u[:], in0=u[:], scalar1=0.044715, scalar2=1.0, op0=ALU.mult, op1=ALU.add), [u], [u])
                P.op('dve', lambda e, y=y, u=u: e.tensor_tensor(out=u[:], in0=u[:], in1=y[:], op=ALU.mult), [u, y], [u])
                P.op('act', lambda e, u=u: e.activation(out=u[:], in_=u[:], func=AF.Sigmoid, scale=1.5957691216057308), [u], [u])
                P.op('dve', lambda e, y=y, u=u: e.tensor_tensor(out=y[:], in0=u[:], in1=y[:], op=ALU.mult), [u, y], [y])
                ya.append(y)
            for j in range(2):
                pp = nps()
                for k in range(2):
                    P.op('pe', lambda e, j=j, k=k, pp=pp: e.matmul(pp[:], lhsT=gluw[:, k, j * 128:(j + 1) * 128], rhs=ya[k][:], start=(k == 0), stop=(k == 1)),
                         [gluw, ya[k]], [pp])
                gt = tmp['d']
                P.op('act', lambda e, j=j, pp=pp, gt=gt: e.activation(out=gt[:], in_=pp[:], func=AF.Sigmoid, bias=vecs[:, V_GLUB + j:V_GLUB + j + 1]), [pp, vecs], [gt])
                P.op('dve', lambda e, j=j, gt=gt: e.tensor_tensor(out=G[:, j, cs], in0=gt[:], in1=ya[j][:], op=ALU.mult), [gt, ya[j]], [(G, j, c)])
            for j in range(2):
                yf, yb, sq = tmp['a'], tmp['b'], tmp['c']
                ld(yf, 1, j, c); ld(yb, 2, j, c)
                P.op('dve', lambda e: e.tensor_tensor(out=yf[:], in0=yf[:], in1=yb[:], op=ALU.add), [yf, yb], [yf])
                P.op('act', lambda e: e.activation(out=sq[:], in_=yf[:], func=AF.Square), [yf], [sq])
                pm, pq = nps(), nps()
                P.op('pe', lambda e, pm=pm: e.matmul(pm[:], lhsT=bones[:], rhs=yf[:], start=True, stop=True), [bones, yf], [pm])
                P.op('pe', lambda e, pq=pq: e.matmul(pq[:], lhsT=bones[:], rhs=sq[:], start=True, stop=True), [bones, sq], [pq])
                msq, rstd = tmp['msq'], tmp['rstd']
                P.op('act', lambda e, pm=pm: e.activation(out=msq[:], in_=pm[:], func=AF.Square), [pm], [msq])
                P.op('dve', lambda e, pq=pq: e.tensor_tensor(out=rstd[:], in0=pq[:], in1=msq[:], op=ALU.subtract), [pq, msq], [rstd])
                P.op('dve', lambda e: e.tensor_scalar(out=rstd[:], in0=rstd[:], scalar1=64e-5, scalar2=None, op0=ALU.add), [rstd], [rstd])
                P.op('act', lambda e: e.activation(out=rstd[:], in_=rstd[:], func=AF.Sqrt), [rstd], [rstd])
                P.op('dve', lambda e: e.reciprocal(out=rstd[:], in_=rstd[:]), [rstd], [rstd])
                P.op('dve', lambda e, pm=pm: e.tensor_tensor(out=yf[:], in0=yf[:], in1=pm[:], op=ALU.subtract), [yf, pm], [yf])
                P.op('dve', lambda e: e.tensor_tensor(out=yf[:], in0=yf[:], in1=rstd[:], op=ALU.mult), [yf, rstd], [yf])
                P.op('act', lambda e, j=j: e.activation(out=yf[:], in_=yf[:], func=AF.Identity, scale=vecs[:, V_GNG + j:V_GNG + j + 1],
                                                         bias=vecs[:, V_GNB + j:V_GNB + j + 1]), [yf, vecs], [yf])
                b1, b2, gg = tmp['d'], tmp['e'], tmp['f']
                ld(b1, 3, j, c); ld(b2, 4, j, c); ld(gg, 5, j, c)
                P.op('dve', lambda e: e.tensor_tensor(out=yf[:], in0=yf[:], in1=b1[:], op=ALU.add), [yf, b1], [yf])
                P.op('dve', lambda e: e.tensor_tensor(out=yf[:], in0=yf[:], in1=b2[:], op=ALU.add), [yf, b2], [yf])
                P.op('dve', lambda e, j=j: e.tensor_tensor(out=G[:, 2 + j, cs], in0=yf[:], in1=gg[:], op=ALU.mult), [yf, gg], [(G, 2 + j, c)])
            for idx, base in ((6, 4), (7, 6)):
                for j in range(2):
                    t = tmp['a' if j == 0 else 'b']
                    ld(t, idx, j, c)
                    slot = base + j
                    P.op('act', lambda e, t=t, slot=slot: e.activation(out=G[:, slot, cs], in_=t[:], func=AF.Copy), [t], [(G, slot, c)])
        for c in range(NCH):
            cs = slice(c * 512, (c + 1) * 512)
            for m in range(8):
                pp = nps()
                for k in range(8):
                    P.op('pe', lambda e, k=k, m=m, pp=pp: e.matmul(pp[:], lhsT=WB[:, k, m * 128:(m + 1) * 128], rhs=G[:, k, cs], start=(k == 0), stop=(k == 7)),
                         [(WB, k), (G, k, c)], [pp])
                P.op('dve', lambda e, m=m, pp=pp: e.scalar_tensor_tensor(out=R[:, m, cs], in0=R[:, m, cs], scalar=ALPHA, in1=pp[:], op0=ALU.mult, op1=ALU.add),
                     [(R, m, c), pp], [(R, m, c)])
        chunks = [(c * 512, 512) for c in range(NCH)]
        layernorm_fm(P, R, chunks, V_L1G, V_L1B, vecs, ones, ps, tmp, out_bf=h1b)
        for grp in range(2):
            for fl in range(11):
                ft = grp * 11 + fl
                wa, wb_ = w13b[(fl % 2) * 2], w13b[(fl % 2) * 2 + 1]
                for wi, (src, dst) in enumerate(((D['fw1'], wa), (D['fw3'], wb_))):
                    s = wst[wi]
                    P.dma(s[:].rearrange("p (k f) -> p k f", k=8), src[:, ft * 128:(ft + 1) * 128].rearrange("(k p) f -> p k f", p=128), writes=[s])
                    P.op('pool', lambda e, s=s, dst=dst: e.tensor_copy(out=dst[:], in_=s[:].rearrange("p (k f) -> p k f", k=8)), [s], [dst])
                for c in range(NCH):
                    cs = slice(c * 512, (c + 1) * 512)
                    pa, pb = nps(), nps()
                    for k in range(8):
                        P.op('pe', lambda e, k=k, pa=pa, wa=wa: e.matmul(pa[:], lhsT=wa[:, k, :], rhs=h1b[:, k, cs], start=(k == 0), stop=(k == 7)),
                             [wa, (h1b, k, c)], [pa])
                    for k in range(8):
                        P.op('pe', lambda e, k=k, pb=pb, wb_=wb_: e.matmul(pb[:], lhsT=wb_[:, k, :], rhs=h1b[:, k, cs], start=(k == 0), stop=(k == 7)),
                             [wb_, (h1b, k, c)], [pb])
                    sa = tmp['a' if c % 2 == 0 else 'b']
                    P.op('act', lambda e, pa=pa, sa=sa: e.activation(out=sa[:], in_=pa[:], func=AF.Silu), [pa], [sa])
                    P.op('dve', lambda e, pb=pb, sa=sa, fl=fl: e.tensor_tensor(out=G[:, fl, cs], in0=pb[:], in1=sa[:], op=ALU.mult), [pb, sa], [(G, fl, c)])
            for fl in range(11):
                ft = grp * 11 + fl
                s = wst[fl % 2]
                P.dma(s[:], D['fw2'][ft * 128:(ft + 1) * 128, :], writes=[s])
                P.op('pool', lambda e, s=s, fl=fl: e.tensor_copy(out=WB[:, fl, :], in_=s[:]), [s], [(WB, fl)])
            for c in range(NCH):
                cs = slice(c * 512, (c + 1) * 512)
                for m in range(8):
                    pp = nps()
                    for fl in range(11):
                        P.op('pe', lambda e, fl=fl, m=m, pp=pp: e.matmul(pp[:], lhsT=WB[:, fl, m * 128:(m + 1) * 128], rhs=G[:, fl, cs], start=(fl == 0), stop=(fl == 10)),
                             [(WB, fl), (G, fl, c)], [pp])
                    if grp == 0:
                        P.op('dve', lambda e, m=m, pp=pp: e.scalar_tensor_tensor(out=R[:, m, cs], in0=R[:, m, cs], scalar=ALPHA, in1=pp[:], op0=ALU.mult, op1=ALU.add),
                             [(R, m, c), pp], [(R, m, c)])
                    else:
                        P.op('dve', lambda e, m=m, pp=pp: e.tensor_tensor(out=R[:, m, cs], in0=R[:, m, cs], in1=pp[:], op=ALU.add),
                             [(R, m, c), pp], [(R, m, c)])
        layernorm_fm(P, R, chunks, V_L2G, V_L2B, vecs, ones, ps, tmp)
        for k in range(8):
            P.dma(D['out'][k * 128:(k + 1) * 128, t0:t0 + NT], R[:, k, :], reads=[(R, k, c) for c in range(NCH)], q=('sp' if k % 2 == 0 else 'pool'))


SEQ = 8192


def fft_consts2():
    n1 = np.arange(128)
    a = 2 * np.pi * np.outer(n1, n1) / 128.0
    C, S = np.cos(a), np.sin(a)
    s = 1.0 / np.sqrt(8192.0)
    cs1 = np.stack([np.concatenate([C, -S], 1), np.concatenate([S, C], 1)], 1) * s
    n2 = np.tile(np.arange(64), 2); cc = np.repeat(np.arange(2), 64)
    k1 = np.arange(128)
    at = 2 * np.pi * np.outer(n2, k1) / 8192.0
    tw = np.stack([np.tile(np.cos(at), (1, 32)), np.tile(np.sin(at), (1, 32))], 1)
    ax = 2 * np.pi * np.outer(n2, n2) / 64.0
    dl = (cc[:, None] == cc[None, :]).astype(np.float64)
    kx = np.stack([np.cos(ax) * dl, np.sin(ax) * dl], 1)
    return cs1.astype(np.float32), tw.astype(np.float32), kx.astype(np.float32)


def fft_phase2(P, D, bufs, ps, q):
    Z, A, B, Tm, cs1, tw, kx, stg = bufs['Z'], bufs['A'], bufs['B'], bufs['Tm'], bufs['cs1'], bufs['tw'], bufs['kx'], bufs['stg']
    P.dma(cs1[:], D['cs1'], writes=[cs1]); P.dma(kx[:], D['kx'], writes=[kx])
    P.dma(tw[:, 0].rearrange('p c k -> p (c k)'), D['tw'][:, 0, :], writes=[tw])
    P.dma(tw[:, 1].rearrange('p c k -> p (c k)'), D['tw'][:, 1, :], writes=[tw], q='pool')
    for ri, r0 in ((0, 1792 + 64 * q), (1, 2048 + 64 * q)):
        for h in range(4):
            src = D['pT'][r0 + 16 * h:r0 + 16 * (h + 1), :].rearrange("c (n1 n2) -> n1 c n2", n2=64)
            dst = Z[:, ri, 8 * h:8 * (h + 1), :].rearrange("p a (c n) -> p (a c) n", c=2)
            P.dma(dst, src, writes=[(Z, ri)], q=('sp' if h % 2 == 0 else 'pool'))
    for cp in range(32):
        pp = ps[cp % 4]
        P.op('pe', lambda e: e.matmul(pp[:, 0:256], lhsT=Z[:, 0, cp, :], rhs=cs1[:, 0, :], start=True, stop=False), [(Z, 0), cs1], [pp])
        P.op('pe', lambda e: e.matmul(pp[:, 0:256], lhsT=Z[:, 1, cp, :], rhs=cs1[:, 1, :], start=False, stop=True), [(Z, 1), cs1], [pp])
        src1 = pp[:, 0:256].rearrange("p (a k) -> p a k", a=2)
        if cp % 2:
            P.op('act', lambda e: e.activation(out=A[:, :, cp, :], in_=src1, func=AF.Copy), [pp], [(A, 0, cp), (A, 1, cp)])
        else:
            P.op('dve', lambda e: e.tensor_copy(out=A[:, :, cp, :], in_=src1), [pp], [(A, 0, cp), (A, 1, cp)])
    allA = [(A, i, cp) for i in range(2) for cp in range(32)]
    tc_b, ts_b = tw[:, 0], tw[:, 1]
    P.op('dve', lambda e: e.tensor_tensor(out=B[:, 0], in0=A[:, 0], in1=tc_b, op=ALU.mult), allA + [tw], [(B, 0)])
    P.op('dve', lambda e: e.tensor_tensor(out=Tm[:], in0=A[:, 1], in1=ts_b, op=ALU.mult), allA + [tw], [Tm])
    P.op('dve', lambda e: e.tensor_tensor(out=B[:, 0], in0=B[:, 0], in1=Tm[:], op=ALU.add), [(B, 0), Tm], [(B, 0)])
    P.op('dve', lambda e: e.tensor_tensor(out=B[:, 1], in0=A[:, 1], in1=tc_b, op=ALU.mult), allA + [tw], [(B, 1)])
    P.op('dve', lambda e: e.tensor_tensor(out=Tm[:], in0=A[:, 0], in1=ts_b, op=ALU.mult), allA + [tw, (B, 0)], [Tm])
    P.op('dve', lambda e: e.tensor_tensor(out=B[:, 1], in0=B[:, 1], in1=Tm[:], op=ALU.subtract), [(B, 1), Tm], [(B, 1)])
    R7 = 7 * 256 + 64 * q
    for g4 in range(8):
        pp = ps[4 + g4 % 4]
        for j in range(4):
            cp = g4 * 4 + j
            P.op('pe', lambda e: e.matmul(pp[:, j * 128:(j + 1) * 128], lhsT=kx[:, 0, :], rhs=B[:, 0, cp, :], start=True, stop=False), [(B, 0), kx], [pp])
            P.op('pe', lambda e: e.matmul(pp[:, j * 128:(j + 1) * 128], lhsT=kx[:, 1, :], rhs=B[:, 1, cp, :], start=False, stop=True), [(B, 1), kx], [pp])
        st_ = stg[g4 % 2]
        if g4 % 2:
            P.op('act', lambda e: e.activation(out=st_[:], in_=pp[:], func=AF.Copy), [pp], [st_])
        else:
            P.op('dve', lambda e: e.tensor_copy(out=st_[:], in_=pp[:]), [pp], [st_])
        for cc in range(2):
            r0 = R7 + 8 * g4 + cc
            dst = D["yin"][r0:r0 + 7:2, :].rearrange("j (k2 k1) -> k2 j k1", k1=128)
            P.dma(dst, st_[cc * 64:(cc + 1) * 64, :].rearrange("p (j k) -> p j k", k=128), reads=[st_], q=('sp' if cc == 0 else 'pool'))


def conv_phase2(P, D, bufs, cw, q):
    a, b, c = bufs['ca'], bufs['cb'], bufs['cc']
    T = SEQ
    P.dma(cw[:], D['cw'][64 * q:64 * (q + 1), :], writes=[cw])
    P.dma(c[:], D['pT'][1024 + 64 * q:1024 + 64 * (q + 1), :], writes=[c]); P.dma(a[:], D['pT'][1280 + 64 * q:1280 + 64 * (q + 1), :], writes=[a], q='pool')
    P.dma(b[:], D['pT'][1536 + 64 * q:1536 + 64 * (q + 1), :], writes=[b])
    P.op('dve', lambda e: e.tensor_tensor(out=a[:], in0=a[:], in1=b[:], op=ALU.mult), [a, b], [a])
    P.op('dve', lambda e: e.tensor_scalar(out=b[:], in0=a[:], scalar1=cw[:, 1:2], scalar2=None, op0=ALU.mult), [a, cw], [b])
    P.op('dve', lambda e: e.scalar_tensor_tensor(out=b[:, 1:T], in0=a[:, 0:T - 1], scalar=cw[:, 0:1], in1=b[:, 1:T], op0=ALU.mult, op1=ALU.add), [a, b, cw], [b])
    P.op('dve', lambda e: e.scalar_tensor_tensor(out=b[:, 0:T - 1], in0=a[:, 1:T], scalar=cw[:, 2:3], in1=b[:, 0:T - 1], op0=ALU.mult, op1=ALU.add), [a, b, cw], [b])
    P.op('dve', lambda e: e.tensor_tensor(out=c[:], in0=c[:], in1=b[:], op=ALU.mult), [c, b], [c])
    P.dma(D['yin'][6 * 256 + 64 * q:6 * 256 + 64 * (q + 1), :], c[:], reads=[c])


T = 8192
NPASS = 13
LB = 16
NBLK = T // LB
NL1 = 4
NL2 = 9


def s5_host(inp, l, q):
    par = np.zeros((128, 4, 3), np.float32)
    bp = np.zeros((128, 8, 64), np.float32)
    cp = np.zeros((128, 8, 128), np.float32)
    dp = np.zeros((64, 128), np.float32)
    for d in range(2):
        for pair in range(2):
            tp = d * 2 + pair
            for gl in range(2):
                gi = 2 * pair + gl
                g = 4 * q + gi
                rows = slice(gl * 64, (gl + 1) * 64)
                par[rows, tp, 0] = inp['s5_lambda_re'][l, d, g]
                par[rows, tp, 1] = inp['s5_lambda_im'][l, d, g]
                par[rows, tp, 2] = inp['s5_log_dt'][l, d, g]
                bp[rows, tp * 2 + 0, gi * 16:(gi + 1) * 16] = inp['s5_b_re'][l, g]
                bp[rows, tp * 2 + 1, gi * 16:(gi + 1) * 16] = inp['s5_b_im'][l, g]
                cp[rows, tp * 2 + 0, 64 + gi * 16:64 + (gi + 1) * 16] = inp['s5_c_re'][l, d, g].T
                cp[rows, tp * 2 + 1, 64 + gi * 16:64 + (gi + 1) * 16] = inp['s5_c_im'][l, d, g].T
    for gi in range(4):
        for h in range(16):
            dp[gi * 16 + h, 64 + gi * 16 + h] = inp['s5_d'][l, 4 * q + gi, h]
    return {'s5par': par, 's5b': bp, 's5c': cp, 's5d': dp}


def s5_prep(P, dram, UY, pst, sm):
    par, bp, cpt, dpd, ident, bbt, cf = sm['par'], sm['bp'], sm['cp'], sm['dp'], sm['ident'], sm['bbt'], sm['cf']
    P.dma(par[:], dram['s5par'], writes=[par]); P.dma(bp[:], dram['s5b'], writes=[bp]); P.dma(cpt[:], dram['s5c'], writes=[cpt], q='pool')
    P.dma(dpd[:], dram['s5d'], writes=[dpd]); P.dma(ident[:], dram['ident'], writes=[ident])
    P.dma(UY[0:64, :], dram['uT'], writes=[(UY, 'u')], q='pool')
    names = ['dt', 'x', 'th', 't2', 's', 'c', 'ex', 'lr', 'li', 'a', 'b', 'den', 'nr', 'cre', 'cim', 'ncim', 'lbr', 'lbi']
    col = {n: i * 4 for i, n in enumerate(names)}
    C = lambda n: cf[:, col[n]:col[n] + 4]
    PW0 = len(names) * 4
    PE0 = PW0 + 156
    pw = lambda kind, j: cf[:, PW0 + kind * 52 + j * 4:PW0 + kind * 52 + j * 4 + 4]

    def dv(fn):
        P.op('dve', fn, [cf, par], [cf])
    lre, lim, ldt = par[:, :, 0], par[:, :, 1], par[:, :, 2]
    P.op('act', lambda e: e.activation(out=C('dt'), in_=ldt, func=AF.Exp), [par], [cf])
    dv(lambda e: e.scalar_tensor_tensor(out=C('x'), in0=lre, scalar=1.0 / 16, in1=C('dt'), op0=ALU.mult, op1=ALU.mult))
    dv(lambda e: e.scalar_tensor_tensor(out=C('th'), in0=lim, scalar=1.0 / 16, in1=C('dt'), op0=ALU.mult, op1=ALU.mult))
    dv(lambda e: e.tensor_tensor(out=C('t2'), in0=C('th'), in1=C('th'), op=ALU.mult))
    sc = [(-1) ** k / math.factorial(2 * k + 1) for k in range(1, 8)]
    cc = [(-1) ** k / math.factorial(2 * k) for k in range(1, 9)]
    for nm, coefs in (('s', sc), ('c', cc)):
        dv(lambda e, nm=nm, c0=coefs[-1]: e.tensor_scalar(out=C(nm), in0=C('t2'), scalar1=c0, scalar2=None, op0=ALU.mult))
        for ck in reversed(coefs[:-1]):
            dv(lambda e, nm=nm, ck=ck: e.scalar_tensor_tensor(out=C(nm), in0=C(nm), scalar=ck, in1=C('t2'), op0=ALU.add, op1=ALU.mult))
    dv(lambda e: e.scalar_tensor_tensor(out=C('s'), in0=C('s'), scalar=1.0, in1=C('th'), op0=ALU.add, op1=ALU.mult))
    dv(lambda e: e.tensor_scalar(out=C('c'), in0=C('c'), scalar1=1.0, scalar2=None, op0=ALU.add))
    dv(lambda e: e.tensor_scalar(out=C('ex'), in0=C('x'), scalar1=1.0 / 5, scalar2=1.0, op0=ALU.mult, op1=ALU.add))
    for k in (4, 3, 2):
        dv(lambda e: e.tensor_tensor(out=C('ex'), in0=C('ex'), in1=C('x'), op=ALU.mult))
        dv(lambda e, k=k: e.tensor_scalar(out=C('ex'), in0=C('ex'), scalar1=1.0 / k, scalar2=1.0, op0=ALU.mult, op1=ALU.add))
    dv(lambda e: e.tensor_tensor(out=C('ex'), in0=C('ex'), in1=C('x'), op=ALU.mult))
    dv(lambda e: e.tensor_scalar(out=C('ex'), in0=C('ex'), scalar1=1.0, scalar2=None, op0=ALU.add))
    dv(lambda e: e.tensor_tensor(out=C('lr'), in0=C('ex'), in1=C('c'), op=ALU.mult))
    dv(lambda e: e.tensor_tensor(out=C('li'), in0=C('ex'), in1=C('s'), op=ALU.mult))

    def csq(sr, si, dr_, di_):
        dv(lambda e: e.tensor_tensor(out=C('a'), in0=si, in1=si, op=ALU.mult))
        dv(lambda e: e.tensor_tensor(out=C('b'), in0=sr, in1=si, op=ALU.mult))
        dv(lambda e: e.tensor_tensor(out=dr_, in0=sr, in1=sr, op=ALU.mult))
        dv(lambda e: e.tensor_tensor(out=dr_, in0=dr_, in1=C('a'), op=ALU.subtract))
        dv(lambda e: e.tensor_scalar(out=di_, in0=C('b'), scalar1=2.0, scalar2=None, op0=ALU.mult))
    csq(C('lr'), C('li'), C('lbr'), C('lbi'))
    csq(C('lbr'), C('lbi'), C('lr'), C('li'))
    csq(C('lr'), C('li'), C('lbr'), C('lbi'))
    csq(C('lbr'), C('lbi'), pw(0, 0), pw(1, 0))
    for j in range(1, NPASS):
        csq(pw(0, j - 1), pw(1, j - 1), pw(0, j), pw(1, j))
    dv(lambda e: e.tensor_scalar(out=cf[:, PW0 + 104:PW0 + 156], in0=cf[:, PW0 + 52:PW0 + 104], scalar1=-1.0, scalar2=None, op0=ALU.mult))
    pe_ = lambda kind, e_: cf[:, PE0 + kind * 4 * LB + (e_ - 1) * 4:PE0 + kind * 4 * LB + e_ * 4]
    dv(lambda e: e.tensor_copy(out=pe_(0, 1), in_=pw(0, 0)))
    dv(lambda e: e.tensor_copy(out=pe_(1, 1), in_=pw(1, 0)))
    for e_ in range(2, LB + 1):
        dv(lambda e: e.tensor_tensor(out=C('a'), in0=pe_(1, e_ - 1), in1=pw(1, 0), op=ALU.mult))
        dv(lambda e: e.tensor_tensor(out=pe_(0, e_), in0=pe_(0, e_ - 1), in1=pw(0, 0), op=ALU.mult))
        dv(lambda e: e.tensor_tensor(out=pe_(0, e_), in0=pe_(0, e_), in1=C('a'), op=ALU.subtract))
        dv(lambda e: e.tensor_tensor(out=C('a'), in0=pe_(1, e_ - 1), in1=pw(0, 0), op=ALU.mult))
        dv(lambda e: e.tensor_tensor(out=pe_(1, e_), in0=pe_(0, e_ - 1), in1=pw(1, 0), op=ALU.mult))
        dv(lambda e: e.tensor_tensor(out=pe_(1, e_), in0=pe_(1, e_), in1=C('a'), op=ALU.add))
    dv(lambda e: e.tensor_scalar(out=cf[:, PE0 + 8 * LB:PE0 + 12 * LB], in0=cf[:, PE0 + 4 * LB:PE0 + 8 * LB], scalar1=-1.0, scalar2=None, op0=ALU.mult))
    dv(lambda e: e.tensor_tensor(out=C('den'), in0=lre, in1=lre, op=ALU.mult))
    dv(lambda e: e.tensor_tensor(out=C('a'), in0=lim, in1=lim, op=ALU.mult))
    dv(lambda e: e.tensor_tensor(out=C('den'), in0=C('den'), in1=C('a'), op=ALU.add))
    dv(lambda e: e.reciprocal(out=C('den'), in_=C('den')))
    dv(lambda e: e.tensor_scalar(out=C('nr'), in0=pw(0, 0), scalar1=-1.0, scalar2=None, op0=ALU.add))
    dv(lambda e: e.tensor_tensor(out=C('a'), in0=C('nr'), in1=lre, op=ALU.mult))
    dv(lambda e: e.tensor_tensor(out=C('b'), in0=pw(1, 0), in1=lim, op=ALU.mult))
    dv(lambda e: e.tensor_tensor(out=C('a'), in0=C('a'), in1=C('b'), op=ALU.add))
    dv(lambda e: e.tensor_tensor(out=C('cre'), in0=C('a'), in1=C('den'), op=ALU.mult))
    dv(lambda e: e.tensor_tensor(out=C('a'), in0=pw(1, 0), in1=lre, op=ALU.mult))
    dv(lambda e: e.tensor_tensor(out=C('b'), in0=C('nr'), in1=lim, op=ALU.mult))
    dv(lambda e: e.tensor_tensor(out=C('a'), in0=C('a'), in1=C('b'), op=ALU.subtract))
    dv(lambda e: e.tensor_tensor(out=C('cim'), in0=C('a'), in1=C('den'), op=ALU.mult))
    dv(lambda e: e.tensor_scalar(out=C('ncim'), in0=C('cim'), scalar1=-1.0, scalar2=None, op0=ALU.mult))
    if 'dbg' in dram:
        P.dma(dram['dbg'], cf[:, PW0:PW0 + 8], reads=[cf])
    for tp in range(4):
        P.op('pool', lambda e, tp=tp: e.tensor_scalar(out=cpt[:, tp * 2 + 1, :], in0=cpt[:, tp * 2 + 1, :], scalar1=-1.0, scalar2=None, op0=ALU.mult), [cpt], [cpt])
    bbw = sm['bbw']
    for tp in range(4):
        cre, cim, ncim = (cf[:, col[n] + tp:col[n] + tp + 1] for n in ('cre', 'cim', 'ncim'))
        br, bi = bp[:, tp * 2, :], bp[:, tp * 2 + 1, :]
        P.op('dve', lambda e: e.tensor_scalar(out=bbw[:, 0, :], in0=br, scalar1=cre, scalar2=None, op0=ALU.mult), [bp, cf], [bbw])
        P.op('dve', lambda e: e.scalar_tensor_tensor(out=bbw[:, 0, :], in0=bi, scalar=ncim, in1=bbw[:, 0, :], op0=ALU.mult, op1=ALU.add), [bp, cf, bbw], [bbw])
        P.op('dve', lambda e: e.tensor_scalar(out=bbw[:, 1, :], in0=bi, scalar1=cre, scalar2=None, op0=ALU.mult), [bp, cf, bbw], [bbw])
        P.op('dve', lambda e: e.scalar_tensor_tensor(out=bbw[:, 1, :], in0=br, scalar=cim, in1=bbw[:, 1, :], op0=ALU.mult, op1=ALU.add), [bp, cf, bbw], [bbw])
        for ri in range(2):
            pp = pst
            P.op('pe', lambda e, ri=ri: e.transpose(out=pp[0:64, 0:128], in_=bbw[:, ri, :], identity=ident[:]), [bbw, ident], [pp])
            P.op('act', lambda e, ri=ri, tp=tp: e.activation(out=bbt[0:64, tp * 2 + ri, :], in_=pp[0:64, 0:128], func=AF.Copy), [pp], [bbt])
    return {'pw': pw, 'cf': cf, 'cpt': cpt, 'bbt': bbt, 'dpd': dpd, 'PE0': PE0}


PC = 2048


def s5_main_gen(P, dram, ctx, RI, UY, psA, psB, sm):
    pw, cf, cpt, bbt, dpd, PE0 = ctx['pw'], ctx['cf'], ctx['cpt'], ctx['bbt'], ctx['dpd'], ctx['PE0']
    Ra, Ia, Rb, Ib = RI
    NBP = PC // LB
    v3 = lambda t_: t_[:, 0:PC].rearrange("p (b s) -> p b s", s=LB)
    S, Ssh = sm['S'], sm['Ssh']
    ykeys = [(UY, 'y', c) for c in range(16)]
    for tp in range(4):
        d = tp // 2
        eb = LB - 1 if d == 0 else 0
        for piece in range(T // PC):
            c0 = piece * PC
            for cc in range(PC // 256):
                cs = slice(c0 + cc * 256, c0 + (cc + 1) * 256)
                ls = slice(cc * 256, (cc + 1) * 256)
                P.op('pe', lambda e: e.matmul(psA[:, 0:256], lhsT=bbt[0:64, tp * 2, :], rhs=UY[0:64, cs], start=True, stop=True), [bbt, (UY, 'u')], [psA])
                P.op('pe', lambda e: e.matmul(psA[:, 256:512], lhsT=bbt[0:64, tp * 2 + 1, :], rhs=UY[0:64, cs], start=True, stop=True), [bbt, (UY, 'u')], [psA])
                P.op('act', lambda e: e.activation(out=Ra[:, ls], in_=psA[:, 0:256], func=AF.Copy), [psA], [Ra])
                P.op('act', lambda e: e.activation(out=Ia[:, ls], in_=psA[:, 256:512], func=AF.Copy), [psA], [Ia])
            yield
            src, dst = (Ra, Ia), (Rb, Ib)
            for j in range(NL1):
                s = 1 << j
                ar, ai, nai = (pw(k, j)[:, tp:tp + 1] for k in (0, 1, 2))
                (sr, si), (dr_, di_) = src, dst
                if d == 0:
                    lo, sh, keep = slice(s, LB), slice(0, LB - s), slice(0, s)
                else:
                    lo, sh, keep = slice(0, LB - s), slice(s, LB), slice(LB - s, LB)
                P.op('dve', lambda e: e.scalar_tensor_tensor(out=v3(dr_)[:, :, lo], in0=v3(sr)[:, :, sh], scalar=ar, in1=v3(sr)[:, :, lo], op0=ALU.mult, op1=ALU.add), [sr, cf], [dr_])
                P.op('dve', lambda e: e.scalar_tensor_tensor(out=v3(dr_)[:, :, lo], in0=v3(si)[:, :, sh], scalar=nai, in1=v3(dr_)[:, :, lo], op0=ALU.mult, op1=ALU.add), [si, cf, dr_], [dr_])
                P.op('dve', lambda e: e.scalar_tensor_tensor(out=v3(di_)[:, :, lo], in0=v3(si)[:, :, sh], scalar=ar, in1=v3(si)[:, :, lo], op0=ALU.mult, op1=ALU.add), [si, cf], [di_])
                P.op('dve', lambda e: e.scalar_tensor_tensor(out=v3(di_)[:, :, lo], in0=v3(sr)[:, :, sh], scalar=ai, in1=v3(di_)[:, :, lo], op0=ALU.mult, op1=ALU.add), [sr, cf, di_], [di_])
                P.op('act', lambda e: e.activation(out=v3(dr_)[:, :, keep], in_=v3(sr)[:, :, keep], func=AF.Copy), [sr, dr_], [dr_])
                P.op('act', lambda e: e.activation(out=v3(di_)[:, :, keep], in_=v3(si)[:, :, keep], func=AF.Copy), [si, di_], [di_])
                src, dst = dst, src
                yield
            fr, fi = src
            P.op('act', lambda e: e.activation(out=S[:, 0, piece * NBP:(piece + 1) * NBP], in_=fr[:, eb:PC:LB], func=AF.Copy), [fr, S], [S])
            P.op('act', lambda e: e.activation(out=S[:, 1, piece * NBP:(piece + 1) * NBP], in_=fi[:, eb:PC:LB], func=AF.Copy), [fi, S], [S])
            for c4 in range(PC // 512):
                c = piece * (PC // 512) + c4
                cs = slice(c * 512, (c + 1) * 512)
                ls = slice(c4 * 512, (c4 + 1) * 512)
                P.op('pe', lambda e: e.matmul(psB[:], lhsT=cpt[:, tp * 2, :], rhs=fr[:, ls], start=True, stop=False), [cpt, fr], [psB])
                P.op('pe', lambda e: e.matmul(psB[:], lhsT=cpt[:, tp * 2 + 1, :], rhs=fi[:, ls], start=False, stop=(tp != 0)), [cpt, fi], [psB])
                if tp == 0:
                    P.op('pe', lambda e: e.matmul(psB[:], lhsT=dpd[:], rhs=UY[0:64, cs], start=False, stop=True), [dpd, (UY, 'u')], [psB])
                    P.op('act', lambda e: e.activation(out=UY[64:128, cs], in_=psB[64:128, :], func=AF.Copy), [psB], [(UY, 'y', c)])
                else:
                    P.op('dve', lambda e: e.tensor_tensor(out=UY[64:128, cs], in0=UY[64:128, cs], in1=psB[64:128, :], op=ALU.add), [psB, (UY, 'y', c)], [(UY, 'y', c)])
            yield
        a_, b_ = 0, 2
        for j in range(NL2):
            s = 1 << j
            ar, ai, nai = (pw(k, NL1 + j)[:, tp:tp + 1] for k in (0, 1, 2))
            if d == 0:
                lo, sh, keep = slice(s, NBLK), slice(0, NBLK - s), slice(0, s)
            else:
                lo, sh, keep = slice(0, NBLK - s), slice(s, NBLK), slice(NBLK - s, NBLK)
            P.op('dve', lambda e: e.scalar_tensor_tensor(out=S[:, b_, lo], in0=S[:, a_, sh], scalar=ar, in1=S[:, a_, lo], op0=ALU.mult, op1=ALU.add), [S, cf], [S])
            P.op('dve', lambda e: e.scalar_tensor_tensor(out=S[:, b_, lo], in0=S[:, a_ + 1, sh], scalar=nai, in1=S[:, b_, lo], op0=ALU.mult, op1=ALU.add), [S, cf], [S])
            P.op('dve', lambda e: e.scalar_tensor_tensor(out=S[:, b_ + 1, lo], in0=S[:, a_ + 1, sh], scalar=ar, in1=S[:, a_ + 1, lo], op0=ALU.mult, op1=ALU.add), [S, cf], [S])
            P.op('dve', lambda e: e.scalar_tensor_tensor(out=S[:, b_ + 1, lo], in0=S[:, a_, sh], scalar=ai, in1=S[:, b_ + 1, lo], op0=ALU.mult, op1=ALU.add), [S, cf], [S])
            P.op('dve', lambda e: e.tensor_copy(out=S[:, b_:b_ + 2, keep], in_=S[:, a_:a_ + 2, keep]), [S], [S])
            a_, b_ = b_, a_
        yield
        P.op('pool', lambda e: e.memset(Ssh[:], 0.0), [Ssh], [Ssh])
        if d == 0:
            P.op('dve', lambda e: e.tensor_copy(out=Ssh[:, :, 1:NBLK], in_=S[:, a_:a_ + 2, 0:NBLK - 1]), [S, Ssh], [Ssh])
        else:
            P.op('dve', lambda e: e.tensor_copy(out=Ssh[:, :, 0:NBLK - 1], in_=S[:, a_:a_ + 2, 1:NBLK]), [S, Ssh], [Ssh])
        for s in range(LB):
            e_ = s + 1 if d == 0 else LB - s
            pr_, pi_ = (cf[:, PE0 + k * 4 * LB + (e_ - 1) * 4 + tp:PE0 + k * 4 * LB + (e_ - 1) * 4 + tp + 1] for k in (0, 1))
            npi_ = cf[:, PE0 + 2 * 4 * LB + (e_ - 1) * 4 + tp:PE0 + 2 * 4 * LB + (e_ - 1) * 4 + tp + 1]
            ct = sm['ct'][s % 2]
            cr_, nci_ = cpt[:, tp * 2, :], cpt[:, tp * 2 + 1, :]
            P.op('pool', lambda e: e.tensor_scalar(out=ct[:, 0, :], in0=cr_, scalar1=pr_, scalar2=None, op0=ALU.mult), [cpt, cf], [ct])
            P.op('dve', lambda e: e.scalar_tensor_tensor(out=ct[:, 0, :], in0=nci_, scalar=pi_, in1=ct[:, 0, :], op0=ALU.mult, op1=ALU.add), [cpt, cf, ct], [ct])
            P.op('pool', lambda e: e.tensor_scalar(out=ct[:, 1, :], in0=nci_, scalar1=pr_, scalar2=None, op0=ALU.mult), [cpt, cf, ct], [ct])
            P.op('dve', lambda e: e.scalar_tensor_tensor(out=ct[:, 1, :], in0=cr_, scalar=npi_, in1=ct[:, 1, :], op0=ALU.mult, op1=ALU.add), [cpt, cf, ct], [ct])
            P.op('pe', lambda e: e.matmul(psB[:, 0:NBLK], lhsT=ct[:, 0, :], rhs=Ssh[:, 0, :], start=True, stop=False), [ct, Ssh], [psB])
            P.op('pe', lambda e: e.matmul(psB[:, 0:NBLK], lhsT=ct[:, 1, :], rhs=Ssh[:, 1, :], start=False, stop=True), [ct, Ssh], [psB])
            P.op('dve', lambda e: e.tensor_tensor(out=UY[64:128, s:T:LB], in0=UY[64:128, s:T:LB], in1=psB[64:128, 0:NBLK], op=ALU.add), [psB] + ykeys, ykeys)
            if s % 4 == 3:
                yield
    P.dma(dram['ys'], UY[64:128, :], reads=ykeys)
    yield


def s5_phase(P, dram, BIG, ps, sm):
    ctx = s5_prep(P, dram, BIG[4], ps[0], sm)
    for _ in s5_main_gen(P, dram, ctx, tuple(b[:, 0:PC] if False else b for b in BIG[:4]), BIG[4], ps[1], ps[5], sm):
        pass


def s5_small_tiles(P):
    return {'par': P.sb([128, 4, 3], F32, 's5par_sb'), 'bp': P.sb([128, 8, 64], F32, 's5b_sb'), 'cp': P.sb([128, 8, 128], F32, 's5c_sb'),
            'dp': P.sb([64, 128], F32, 's5d_sb'), 'ident': P.sb([128, 128], F32, 'ident_sb'), 'bbt': P.sb([64, 8, 128], F32, 'bbt'),
            'cf': P.sb([128, 18 * 4 + 156 + 12 * LB], F32, 'cf'), 'bbw': P.sb([128, 2, 64], F32, 'bbw'),
            'S': P.sb([128, 4, NBLK], F32, 's5S'), 'Ssh': P.sb([128, 2, NBLK], F32, 's5Ssh'), 'ct': [P.sb([128, 2, 128], F32, 's5ct%d' % i) for i in range(2)]}


T = 8192
TP = 1024
NPIECE = T // TP
L = 64
NB = 4
DECAY = 0.6065306597126334


def rwkv_consts():
    s = np.arange(64)[:, None]; t = np.arange(64)[None, :]
    su = (s < t).astype(np.float32); sl = (s > t).astype(np.float32); si = (s <= t).astype(np.float32)
    m = np.zeros((64, 6, 256), np.float32)
    m[:, 0] = np.tile(su, (1, 4)); m[:, 1] = np.tile(sl, (1, 4))
    m[:, 2] = np.tile(si, (1, 4)); m[:, 3] = np.tile(si, (1, 4))
    m[:, 4] = np.tile(su, (1, 4))
    m[:, 5] = np.tile(np.eye(64, dtype=np.float32), (1, 4))
    b1 = np.zeros((128, 128), np.float32); b1[:64, :64] = 1; b1[64:, 64:] = 1
    cm = np.ones((128, TP), np.float32); cm[:, 0::64] = 0
    return {'rmask': m, 'bones1': b1, 'cmask': cm, 'ident': np.eye(128, dtype=np.float32)}


def rwkv_host(inp, l, q, pT, lor):
    def both(rows):
        return np.concatenate([rows, rows[:, ::-1]], 0)
    rkv = np.stack([both(pT[256 + j * 256 + 64 * q:256 + j * 256 + 64 * q + 64]) for j in range(3)])
    sg = np.concatenate([lor[64 * q:64 * q + 64], lor[256 + 64 * q:256 + 64 * q + 64][:, ::-1]], 0)
    a = np.concatenate([lor[512 + 64 * q:512 + 64 * q + 64], lor[768 + 64 * q:768 + 64 * q + 64][:, ::-1]], 0)
    par = np.zeros((128, 16), np.float32)
    for d in range(2):
        rows = slice(d * 64, (d + 1) * 64)
        for j in range(3):
            par[rows, j] = inp['rwkv_mu_rkv'][l, d, j * 256 + 64 * q:j * 256 + 64 * q + 64]
        par[rows, 3] = inp['rwkv_k_k'][l, 64 * q:64 * q + 64]
        par[rows, 4] = inp['rwkv_k_a'][l, 64 * q:64 * q + 64]
        par[rows, 5] = inp['rwkv_r_k'][l, q]
    return {'rkv': np.ascontiguousarray(rkv), 'sga': np.ascontiguousarray(np.stack([sg, a])), 'rpar': par}


def rwkv_gen(P, dram, X, ps, W_, sm, share_chain_bank=False):
    rpar, rmask, bones1, cmask, ident = sm['rpar'], sm['rmask'], sm['bones1'], sm['cmask'], sm['ident']
    Hblk, Ublk, RHSs, yst, gL, prevc, tmpc = sm['Hblk'], sm['Ublk'], sm['RHSs'], sm['yst'], sm['gL'], sm['prevc'], sm['tmpc']
    PQ, PW = [ps[0], ps[1]], [ps[2], ps[3]]
    RHSp, Up, Yp, Hp = (ps[4], ps[4], ps[5], ps[4]) if share_chain_bank else (ps[4], ps[5], ps[6], ps[7])
    QP, Wk, Wfin, Mak, Mr, Tv, Tb, Tk = W_['QP'], W_['W'], W_['Wfin'], W_['Mak'], W_['Mr'], W_['Tv'], W_['Tb'], W_['Tk']
    P.dma(rpar[:, 0:16], dram['rpar'], writes=[rpar]); P.dma(rmask[:], dram['rmask'], writes=[rmask]); P.dma(bones1[:], dram['bones1'], writes=[bones1])
    P.dma(cmask[:], dram['cmask'], writes=[cmask], q='pool')
    P.dma(ident[:], dram['ident'], writes=[ident])
    pc = lambda i: rpar[:, i:i + 1]
    P.op('dve', lambda e: e.tensor_scalar(out=rpar[:, 6:9], in0=rpar[:, 0:3], scalar1=-1.0, scalar2=1.0, op0=ALU.mult, op1=ALU.add), [rpar], [rpar])
    P.op('dve', lambda e: e.tensor_scalar(out=rpar[:, 9:10], in0=rpar[:, 4:5], scalar1=-1.0, scalar2=1.0, op0=ALU.mult, op1=ALU.add), [rpar], [rpar])
    for t_ in (Hblk, Ublk) + tuple(Tv) + tuple(Tb) + tuple(Tk):
        P.op('pool', lambda e, t_=t_: e.memset(t_[:], 0.0), [], [t_])

    for piece in range(NPIECE):
        t0 = piece * TP
        X0, X1, X2, X3, X4, X5, X6, X7, X8 = X[:9]
        Atb, Rtb, Btb, Ktb = X[9:13]
        fused = 'q' in dram
        if fused:
            q_ = dram['q']; pTd, lord, yind = dram['pT'], dram['lor'], dram['yin']
            m0 = T - t0 - TP

            def load2(dst, src0, src1):
                P.dma(dst[0:64, :], src0[:, t0:t0 + TP], writes=[dst])
                P.dma(X8[64:128, :], src1[:, m0:m0 + TP], writes=[X8], q='pool')
                P.op('act', lambda e: e.activation(out=dst[64:128, :], in_=X8[64:128, ::-1], func=AF.Copy), [X8, dst], [dst])
            for j, dst in enumerate((X0, X1, X2)):
                rows = pTd[256 + j * 256 + 64 * q_:256 + j * 256 + 64 * (q_ + 1), :]
                load2(dst, rows, rows)
                if piece == 0:
                    P.op('pool', lambda e, j=j: e.memset(prevc[:, j:j + 1], 0.0), [], [prevc])
                else:
                    P.dma(prevc[0:64, j:j + 1], rows[:, t0 - 1:t0], writes=[prevc], allow_slow_non_contiguous=True)
                    P.dma(prevc[64:128, j:j + 1], rows[:, T - t0:T - t0 + 1], writes=[prevc], allow_slow_non_contiguous=True)
            load2(X3, lord[64 * q_:64 * (q_ + 1), :], lord[256 + 64 * q_:256 + 64 * (q_ + 1), :])
            load2(X4, lord[512 + 64 * q_:512 + 64 * (q_ + 1), :], lord[768 + 64 * q_:768 + 64 * (q_ + 1), :])
        else:
            for j, dst in enumerate((X0, X1, X2)):
                P.dma(dst[:], dram['rkv'][j, :, t0:t0 + TP], writes=[dst], q=('sp' if j % 2 == 0 else 'pool'))
                if piece == 0:
                    P.op('pool', lambda e, j=j: e.memset(prevc[:, j:j + 1], 0.0), [], [prevc])
                else:
                    P.dma(prevc[:, j:j + 1], dram['rkv'][j, :, t0 - 1:t0], writes=[prevc], allow_slow_non_contiguous=True)
            P.dma(X3[:], dram['sga'][0, :, t0:t0 + TP], writes=[X3]); P.dma(X4[:], dram['sga'][1, :, t0:t0 + TP], writes=[X4], q='pool')
        for j, (src, dst) in enumerate(((X0, X5), (X1, X6), (X2, X7))):
            P.op('dve', lambda e: e.tensor_scalar(out=dst[:], in0=src[:], scalar1=pc(6 + j), scalar2=None, op0=ALU.mult), [src, rpar], [dst])
            P.op('dve', lambda e: e.scalar_tensor_tensor(out=dst[:, 1:TP], in0=src[:, 0:TP - 1], scalar=pc(j), in1=dst[:, 1:TP], op0=ALU.mult, op1=ALU.add), [src, rpar, dst], [dst])
            P.op('dve', lambda e: e.scalar_tensor_tensor(out=dst[:, 0:1], in0=prevc[:, j:j + 1], scalar=pc(j), in1=dst[:, 0:1], op0=ALU.mult, op1=ALU.add), [prevc, rpar, dst], [dst])
        yield
        P.op('dve', lambda e: e.tensor_scalar(out=X0[:], in0=X6[:], scalar1=pc(3), scalar2=None, op0=ALU.mult), [X6, rpar, prevc], [X0])
        for c in range(TP // 512):
            cs = slice(c * 512, (c + 1) * 512)
            pp = PQ[c % 2]
            P.op('act', lambda e: e.activation(out=tmpc[:], in_=X0[:, cs], func=AF.Square), [X0], [tmpc])
            P.op('pe', lambda e: e.matmul(pp[:], lhsT=bones1[:], rhs=tmpc[:], start=True, stop=True), [bones1, tmpc], [pp])
            P.op('dve', lambda e: e.tensor_scalar(out=tmpc[:], in0=pp[:], scalar1=1e-12, scalar2=None, op0=ALU.add), [pp], [tmpc])
            P.op('act', lambda e: e.activation(out=tmpc[:], in_=tmpc[:], func=AF.Sqrt), [tmpc], [tmpc])
            P.op('dve', lambda e: e.reciprocal(out=tmpc[:], in_=tmpc[:]), [tmpc], [tmpc])
            P.op('dve', lambda e: e.tensor_tensor(out=X0[:, cs], in0=X0[:, cs], in1=tmpc[:], op=ALU.mult), [X0, tmpc], [X0])
        yield
        P.op('dve', lambda e: e.tensor_scalar(out=X1[:], in0=X4[:], scalar1=pc(4), scalar2=pc(9), op0=ALU.mult, op1=ALU.add), [X4, rpar, prevc], [X1])
        P.op('dve', lambda e: e.tensor_tensor(out=X6[:], in0=X6[:], in1=X1[:], op=ALU.mult), [X6, X1], [X6])
        P.op('dve', lambda e: e.tensor_tensor(out=X4[:], in0=X0[:], in1=X4[:], op=ALU.mult), [X0, X4], [X4])
        P.op('dve', lambda e: e.scalar_tensor_tensor(out=X1[:], in0=X5[:], scalar=pc(5), in1=X6[:], op0=ALU.mult, op1=ALU.mult), [X5, X6, rpar], [X1])
        for c in range(TP // 512):
            cs = slice(c * 512, (c + 1) * 512)
            pp = PW[c % 2]
            P.op('pe', lambda e: e.matmul(pp[:], lhsT=bones1[:], rhs=X1[:, cs], start=True, stop=True), [bones1, X1], [pp])
            P.op('dve', lambda e: e.tensor_tensor(out=X2[:, cs], in0=pp[:], in1=X7[:, cs], op=ALU.mult), [pp, X7, prevc], [X2])
        if fused:
            P.dma(yind[3 * 256 + 64 * q_:3 * 256 + 64 * (q_ + 1), t0:t0 + TP], X2[0:64, :], reads=[X2])
            P.op('act', lambda e: e.activation(out=X8[64:128, :], in_=X2[64:128, ::-1], func=AF.Copy), [X2, X8], [X8])
            P.dma(yind[4 * 256 + 64 * q_:4 * 256 + 64 * (q_ + 1), m0:m0 + TP], X8[64:128, :], reads=[X8], q='pool')
        else:
            P.dma(dram['bo'][:, t0:t0 + TP], X2[:], reads=[X2])
        yield
        P.op('act', lambda e: e.activation(out=X3[:], in_=X3[:], func=AF.Copy, scale=-DECAY), [X3], [X3])
        P.op('dve', lambda e: e.tensor_tensor_scan(out=X3[:], data0=cmask[:], data1=X3[:], initial=0.0, op0=ALU.mult, op1=ALU.add), [X3, cmask], [X3])
        P.op('act', lambda e: e.activation(out=gL[:], in_=X3[:, L - 1:TP:L], func=AF.Exp), [X3], [gL])
        P.op('act', lambda e: e.activation(out=X1[:], in_=X3[:], func=AF.Exp), [X3], [X1])
        P.op('pool', lambda e: e.tensor_copy(out=X8[:, 1:TP], in_=X1[:, 0:TP - 1]), [X1], [X8])
        P.op('pool', lambda e: e.memset(X8[:, 0:TP:L], 1.0), [X8], [X8])
        P.op('dve', lambda e: e.scalar_tensor_tensor(out=X0[:], in0=X0[:], scalar=-1.0, in1=X8[:], op0=ALU.mult, op1=ALU.mult), [X0, X8], [X0])
        P.op('dve', lambda e: e.tensor_tensor(out=X5[:], in0=X5[:], in1=X1[:], op=ALU.mult), [X5, X1], [X5])
        P.op('act', lambda e: e.activation(out=X1[:], in_=X3[:], func=AF.Exp, scale=-1.0), [X3, X5], [X1])
        P.op('dve', lambda e: e.tensor_tensor(out=X4[:], in0=X4[:], in1=X1[:], op=ALU.mult), [X4, X1], [X4])
        P.op('dve', lambda e: e.tensor_tensor(out=X6[:], in0=X6[:], in1=X1[:], op=ALU.mult), [X6, X1], [X6])
        At, Rt, Bt, Kt, Vt = X0, X5, X4, X6, X7
        for i_, (src_, dst_) in enumerate(((At, Atb), (Rt, Rtb), (Bt, Btb), (Kt, Ktb))):
            if i_ % 2:
                P.op('pool', lambda e: e.tensor_copy(out=dst_[:], in_=src_[:]), [src_], [dst_])
            else:
                P.op('act', lambda e: e.activation(out=dst_[:], in_=src_[:], func=AF.Copy), [src_], [dst_])

        nbatch = TP // (L * NB)

        def pre_slices(b):
            bs = b % 2
            cols = [slice((b * NB + i) * L, (b * NB + i + 1) * L) for i in range(NB)]
            dr = [slice(0, 64), slice(64, 128)]
            sl = []

            def a1():
                for d in range(2):
                    for i in range(NB):
                        P.op('pe', lambda e: e.matmul(PQ[d][0:64, i * 64:(i + 1) * 64], lhsT=Btb[dr[d], cols[i]], rhs=Atb[dr[d], cols[i]], start=True, stop=True), [Btb, Atb], [PQ[d]])
                        P.op('pe', lambda e: e.matmul(PQ[d][0:64, 256 + i * 64:256 + (i + 1) * 64], lhsT=Atb[dr[d], cols[i]], rhs=Btb[dr[d], cols[i]], start=True, stop=True), [Btb, Atb], [PQ[d]])
                    P.op('dve', lambda e: e.tensor_tensor(out=QP[0][:, d, :], in0=PQ[d][0:64, :], in1=rmask[:, 0:2, :].rearrange("p a m -> p (a m)"), op=ALU.mult), [PQ[d], rmask], [QP[0]])
                P.op('pool', lambda e: e.tensor_tensor(out=Wk[0][:, 0, :], in0=QP[0][:, 0, 0:256], in1=rmask[:, 5, :], op=ALU.add), [QP[0], rmask], [Wk[0]])
                P.op('pool', lambda e: e.tensor_tensor(out=Wk[0][:, 1, :], in0=QP[0][:, 1, 0:256], in1=rmask[:, 5, :], op=ALU.add), [QP[0], rmask, Wk[0]], [Wk[0]])
            sl.append(a1)

            def a2():
                for d in range(2):
                    for i in range(NB):
                        P.op('pe', lambda e: e.matmul(PW[d][0:64, i * 64:(i + 1) * 64], lhsT=Ktb[dr[d], cols[i]], rhs=Atb[dr[d], cols[i]], start=True, stop=True), [Ktb, Atb], [PW[d]])
                    P.op('dve', lambda e: e.tensor_tensor(out=Mak[bs][:, d, :], in0=PW[d][0:64, 0:256], in1=rmask[:, 4, :], op=ALU.mult), [PW[d], rmask], [Mak[bs]])
            sl.append(a2)

            def a3():
                for d in range(2):
                    for i in range(NB):
                        P.op('pe', lambda e: e.matmul(PQ[d][0:64, i * 64:(i + 1) * 64], lhsT=Btb[dr[d], cols[i]], rhs=Rtb[dr[d], cols[i]], start=True, stop=True), [Btb, Rtb], [PQ[d]])
                        P.op('pe', lambda e: e.matmul(PQ[d][0:64, 256 + i * 64:256 + (i + 1) * 64], lhsT=Ktb[dr[d], cols[i]], rhs=Rtb[dr[d], cols[i]], start=True, stop=True), [Ktb, Rtb], [PQ[d]])
                    P.op('dve', lambda e: e.tensor_tensor(out=Mr[bs][:, d, :], in0=PQ[d][0:64, :], in1=rmask[:, 2:4, :].rearrange("p a m -> p (a m)"), op=ALU.mult), [PQ[d], rmask], [Mr[bs]])
            sl.append(a3)

            def tr(src, dstT, pbank):
                def f():
                    for i in range(NB):
                        P.op('pe', lambda e: e.transpose(out=pbank[0:64, i * 128:(i + 1) * 128], in_=src[:, cols[i]], identity=ident[:]), [src, ident], [pbank])
                    v = pbank[0:64, :].rearrange("p (i m) -> p i m", m=128)
                    P.op('act', lambda e: e.activation(out=dstT[bs][:, :, 0, 0:64], in_=v[:, :, 0:64], func=AF.Copy), [pbank], [dstT[bs]])
                    P.op('act', lambda e: e.activation(out=dstT[bs][:, :, 1, 64:128], in_=v[:, :, 64:128], func=AF.Copy), [pbank, dstT[bs]], [dstT[bs]])
                return f
            sl.append(tr(Vt, Tv, PW[0])); sl.append(tr(Bt, Tb, PW[1])); sl.append(tr(Kt, Tk, PW[0]))
            for j in range(5):
                k = j % 2

                def qp(j=j, k=k):
                    for d in range(2):
                        for i in range(NB):
                            q_ = QP[k][:, d, i * 64:(i + 1) * 64]; p_ = QP[k][:, d, 256 + i * 64:256 + (i + 1) * 64]
                            P.op('pe', lambda e: e.matmul(PQ[d][0:64, i * 64:(i + 1) * 64], lhsT=p_, rhs=q_, start=True, stop=True), [QP[k]], [PQ[d]])
                            P.op('pe', lambda e: e.matmul(PQ[d][0:64, 256 + i * 64:256 + (i + 1) * 64], lhsT=q_, rhs=p_, start=True, stop=True), [QP[k]], [PQ[d]])
                        P.op('act', lambda e: e.activation(out=QP[1 - k][:, d, :], in_=PQ[d][0:64, :], func=AF.Copy), [PQ[d]], [QP[1 - k]])
                sl.append(qp)

                def wu(j=j, k=k):
                    dstW = Wfin[bs] if j == 4 else Wk[1 - k]
                    for d in range(2):
                        for i in range(NB):
                            P.op('pe', lambda e: e.matmul(PW[d][0:64, i * 64:(i + 1) * 64], lhsT=QP[1 - k][:, d, 256 + i * 64:256 + (i + 1) * 64], rhs=Wk[k][:, d, i * 64:(i + 1) * 64],
                                                          start=True, stop=True), [QP[1 - k], Wk[k]], [PW[d]])
                        P.op('dve', lambda e: e.tensor_tensor(out=dstW[:, d, :], in0=PW[d][0:64, 0:256], in1=Wk[k][:, d, :], op=ALU.add), [PW[d], Wk[k]], [dstW])
                sl.append(wu)
            return sl

        def seq_chunk(b, i):
            bs = b % 2
            c = b * NB + i
            cs = slice(c * L, (c + 1) * L)
            blk = slice(i * 64, (i + 1) * 64)
            P.op('pe', lambda e: e.matmul(RHSp[0:64, 0:128], lhsT=At[:, cs], rhs=Hblk[:], start=True, stop=False), [At, Hblk], [RHSp])
            for d in range(2):
                P.op('pe', lambda e: e.matmul(RHSp[0:64, d * 64:(d + 1) * 64], lhsT=Mak[bs][:, d, blk], rhs=Tv[bs][:, i, d, d * 64:(d + 1) * 64], start=False, stop=(d == 1)),
                     [Mak[bs], Tv[bs]], [RHSp])
            P.op('act', lambda e: e.activation(out=RHSs[:], in_=RHSp[0:64, 0:128], func=AF.Copy), [RHSp], [RHSs])
            for d in range(2):
                P.op('pe', lambda e: e.matmul(Up[0:64, d * 64:(d + 1) * 64], lhsT=Wfin[bs][:, d, blk], rhs=RHSs[:, d * 64:(d + 1) * 64], start=True, stop=True), [Wfin[bs], RHSs], [Up])
            P.op('act', lambda e: e.activation(out=Ublk[:, 0, 0:64], in_=Up[0:64, 0:64], func=AF.Copy), [Up], [Ublk])
            P.op('act', lambda e: e.activation(out=Ublk[:, 1, 64:128], in_=Up[0:64, 64:128], func=AF.Copy), [Up, Ublk], [Ublk])
            P.op('pe', lambda e: e.matmul(Yp[:, 0:64], lhsT=Hblk[:], rhs=Rt[:, cs], start=True, stop=False), [Hblk, Rt], [Yp])
            for d in range(2):
                P.op('pe', lambda e: e.matmul(Yp[:, 0:64], lhsT=Ublk[:, d, :], rhs=Mr[bs][:, d, blk], start=False, stop=False), [Ublk, Mr[bs]], [Yp])
            for d in range(2):
                P.op('pe', lambda e: e.matmul(Yp[:, 0:64], lhsT=Tv[bs][:, i, d, :], rhs=Mr[bs][:, d, 256 + i * 64:256 + (i + 1) * 64], start=False, stop=(d == 1)), [Tv[bs], Mr[bs]], [Yp])
            P.op('act', lambda e: e.activation(out=yst[:, (c % 8) * 64:(c % 8 + 1) * 64], in_=Yp[:, 0:64], func=AF.Copy), [Yp], [yst])
            if c % 8 == 7:
                if 'q' in dram:
                    q_ = dram['q']; yind = dram['yin']; tp0 = t0 + (c - 7) * L
                    P.dma(yind[256 + 64 * q_:256 + 64 * (q_ + 1), tp0:tp0 + 512], yst[0:64, :], reads=[yst])
                    ystf = sm['ystf']
                    P.op('act', lambda e: e.activation(out=ystf[64:128, :], in_=yst[64:128, ::-1], func=AF.Copy), [yst, ystf], [ystf])
                    P.dma(yind[2 * 256 + 64 * q_:2 * 256 + 64 * (q_ + 1), T - tp0 - 512:T - tp0], ystf[64:128, :], reads=[ystf], q='pool')
                else:
                    P.dma(dram['yo'][:, t0 + (c - 7) * L:t0 + (c + 1) * L], yst[:], reads=[yst])
            P.op('pe', lambda e: e.matmul(Hp[:, 0:128], lhsT=ident[:], rhs=Hblk[:], start=True, stop=False), [ident, Hblk], [Hp])
            for d in range(2):
                P.op('pe', lambda e: e.matmul(Hp[:, 0:128], lhsT=Tb[bs][:, i, d, :], rhs=Ublk[:, d, :], start=False, stop=False), [Tb[bs], Ublk], [Hp])
            for d in range(2):
                P.op('pe', lambda e: e.matmul(Hp[:, 0:128], lhsT=Tk[bs][:, i, d, :], rhs=Tv[bs][:, i, d, :], start=False, stop=(d == 1)), [Tk[bs], Tv[bs]], [Hp])
            P.op('act', lambda e: e.activation(out=Hblk[:], in_=Hp[:, 0:128], func=AF.Copy, scale=gL[:, c:c + 1]), [Hp, gL], [Hblk])

        yield
        cur = pre_slices(0)
        for f in cur:
            f()
            yield
        for b in range(nbatch):
            nxt = pre_slices(b + 1) if b + 1 < nbatch else []
            per = (len(nxt) + NB - 1) // NB
            for i in range(NB):
                seq_chunk(b, i)
                for f in nxt[i * per:(i + 1) * per]:
                    f()
                yield


def rwkv_phase(P, dram, X, ps, W_, sm):
    for _ in rwkv_gen(P, dram, X, ps, W_, sm):
        pass


class V:
    def __init__(self, ap, name):
        self.ap = ap; self.name = name

    def __getitem__(self, idx):
        return self.ap[idx]


def rwkv_tiles(P, BIG3, BIG4):
    off = {'a': 0, 'b': 0}

    def take(buf, key, n):
        o = off[key]; off[key] += n
        return buf[0:64, o:o + n]
    W_ = {}
    for nm, n in (('QP', 512), ('W', 256)):
        W_[nm] = [V(take(BIG3, 'a', n).bitcast(BF16).rearrange("p (d m) -> p d m", d=2), 'rw_%s%d' % (nm, i)) for i in range(2)]
    for nm, n in (('Wfin', 512), ('Mak', 512), ('Mr', 1024)):
        W_[nm] = [V(take(BIG3, 'a', n).rearrange("p (d m) -> p d m", d=2), 'rw_%s%d' % (nm, i)) for i in range(2)]
    for nm in ('Tv', 'Tb', 'Tk'):
        W_[nm] = [V(take(BIG4, 'b', 1024).rearrange("p (i d m) -> p i d m", i=4, d=2), 'rw_%s%d' % (nm, i)) for i in range(2)]
    return W_


def rwkv_X(BIGS):
    X = []
    for i in range(9):
        b = BIGS[i // 4]; o = (i % 4) * TP
        X.append(V(b[:, o:o + TP], 'rw_X%d' % i))
    base = 2 * 4 * TP + TP
    for i in range(4):
        o = 4096 + 2048 + i * (TP // 2)
        X.append(V(BIGS[2][:, o:o + TP // 2].bitcast(BF16), 'rw_Xb%d' % i))
    return X


def rwkv_small(P, cmask=None):
    return {'rpar': P.sb([128, 16], F32, 'rpar_sb'), 'rmask': P.sb([64, 6, 256], F32, 'rmask_sb'), 'bones1': P.sb([128, 128], F32, 'bones1_sb'),
            'cmask': cmask if cmask is not None else P.sb([128, TP], F32, 'cmask_sb'), 'ident': P.sb([128, 128], F32, 'identr_sb'), 'Hblk': P.sb([128, 128], F32, 'Hblk'),
            'Ublk': P.sb([64, 2, 128], F32, 'Ublk'), 'RHSs': P.sb([64, 128], F32, 'RHSs'), 'yst': P.sb([128, 512], F32, 'yst'),
            'gL': P.sb([128, TP // L], F32, 'gL'), 'ystf': P.sb([128, 512], F32, 'ystf'), 'prevc': P.sb([128, 4], F32, 'prevc'), 'tmpc': P.sb([128, 512], F32, 'tmpc')}


def rwkv_alloc(P):
    X = [P.sb([128, TP], F32, 'rwX%d' % i) for i in range(9)] + [P.sb([128, TP], BF16, 'rwXb%d' % i) for i in range(4)]
    W_ = {}
    for nm, n, dt_ in (('QP', 512, BF16), ('W', 256, BF16), ('Wfin', 256, F32), ('Mak', 256, F32), ('Mr', 512, F32)):
        W_[nm] = [P.sb([64, 2, n], dt_, 'rw_%s%d' % (nm, i)) for i in range(2)]
    for nm in ('Tv', 'Tb', 'Tk'):
        W_[nm] = [P.sb([64, 4, 2, 128], F32, 'rw_%s%d' % (nm, i)) for i in range(2)]
    return X, W_


from concourse.bass_utils import run_bass_kernel_spmd

SEQ = 8192


def kb_phase1(P, D, ps, l):
    BIG = [P.sb([128, SEQ], F32, 'BIG%d' % i) for i in range(5)]
    cs1 = P.sb([128, 2, 256], F32, 'cs1s'); kx = P.sb([128, 2, 128], F32, 'kxs'); cw = P.sb([64, 3], F32, 'cws')
    stg = [P.sb([128, 512], F32, 'stg%d' % i) for i in range(2)]
    v4 = lambda b: b[:].rearrange("p (a c k) -> p a c k", a=2, c=32)
    for q in range(4):
        bufs = {'Z': v4(BIG[0]), 'A': v4(BIG[1]), 'B': v4(BIG[2]), 'Tm': BIG[3][:, 0:4096].rearrange("p (c k) -> p c k", c=32),
                'cs1': cs1, 'tw': v4(BIG[4]), 'kx': kx, 'stg': stg}
        fft_phase2(P, D, bufs, ps, q)
        P.barrier()
        conv_phase2(P, D, {'ca': BIG[4][0:64, :], 'cb': BIG[0][0:64, :], 'cc': BIG[1][0:64, :]}, cw, q)
        P.barrier()


def kb_phase2(P, D, ps, l):
    UY = P.sb([128, SEQ], F32, 's5UY')
    RI = tuple(P.sb([128, PC], F32, 's5RI%d' % i) for i in range(4))
    s5sm = s5_small_tiles(P)
    rsm = rwkv_small(P)
    X, W_ = rwkv_alloc(P)
    for q in range(4):
        dq = {'s5par': D['s5par'][q * 128:(q + 1) * 128], 's5b': D['s5b'][q * 128:(q + 1) * 128], 's5c': D['s5c'][q * 128:(q + 1) * 128],
              's5d': D['s5d'][q * 64:(q + 1) * 64], 'ident': D['ident'], 'uT': D['pT'][64 * q:64 * (q + 1), :], 'ys': D['yin'][64 * q:64 * (q + 1), :]}
        ctx = s5_prep(P, dq, UY, ps[6], s5sm)
        gs = s5_main_gen(P, dq, ctx, RI, UY, ps[6], ps[7], s5sm)
        dr_ = dict(D); dr_['q'] = q; dr_['rpar'] = D['rpar'][q * 128:(q + 1) * 128]
        gr = rwkv_gen(P, dr_, X, ps, W_, rsm, share_chain_bank=True)
        alive = [gr, gs]
        while alive:
            for g in list(alive):
                try:
                    next(g)
                except StopIteration:
                    alive.remove(g)
        P.barrier()


def build_fused():
    nc = bass.Bass("TRN2", target_bir_lowering=False)
    dr = lambda n, s, kind="ExternalInput": nc.dram_tensor(n, s, F32, kind=kind).ap()
    it = lambda n, s: nc.dram_tensor(n, s, F32).ap()
    xT = dr('xT', [1024, SEQ]); outT = dr('outT', [1024, SEQ], "ExternalOutput")
    h0 = it('h0_i', [1024, SEQ]); h1 = it('h1_i', [1024, SEQ]); pT = it('pT_i', [2304, SEQ]); lor = it('lor_i', [1280, SEQ]); yin = it('yin_i', [2048, SEQ])
    shared = {'d64': dr('d64', [256, 512]), 'cs1': dr('cs1', [128, 2, 256]), 'tw': dr('tw', [128, 2, 4096]), 'kx': dr('kx', [128, 2, 128]),
              'ident': dr('ident', [128, 128]), 'rmask': dr('rmask', [64, 6, 256]), 'bones1': dr('bones1', [128, 128]), 'cmask': dr('cmask', [128, TP])}
    per = []
    for l in range(2):
        sfx = '_%d' % l
        per.append({'vecsA': dr('vecsA' + sfx, [128, NVA]), 'w_in': dr('w_in' + sfx, [1024, 2048]), 'wfT': dr('wfT' + sfx, [256, 1024]),
                    'w1': dr('w1' + sfx, [2, 1024, 64]), 'a1': dr('a1' + sfx, [2, 1024, 64]), 'g1': dr('g1' + sfx, [1024, 160]),
                    'w2': dr('w2' + sfx, [128, 256]), 'a2': dr('a2' + sfx, [128, 256]), 'g2': dr('g2' + sfx, [160, 256]),
                    'cw': dr('cw' + sfx, [256, 3]), 's5par': dr('s5par' + sfx, [512, 4, 3]), 's5b': dr('s5b' + sfx, [512, 8, 64]),
                    's5c': dr('s5c' + sfx, [512, 8, 128]), 's5d': dr('s5d' + sfx, [256, 128]), 'rpar': dr('rpar' + sfx, [512, 16]),
                    'vecs': dr('vecs' + sfx, [128, NV]), 'glu_w': dr('glu_w' + sfx, [256, 256]), 'w_out': dr('w_out' + sfx, [1024, 1024]),
                    'fw1': dr('fw1' + sfx, [1024, 2816]), 'fw3': dr('fw3' + sfx, [1024, 2816]), 'fw2': dr('fw2' + sfx, [2816, 1024])})
    with contextlib.ExitStack() as st:
        P = Prog(nc, st)
        ps = [P.ps([128, 512], F32, 'ps%d' % i) for i in range(8)]
        for l in range(2):
            D = dict(shared); D.update(per[l]); D.update({'pT': pT, 'lor': lor, 'yin': yin, 'h0': h0})
            with contextlib.ExitStack() as s_a:
                P.stack = s_a; P.suffix = '_a%d' % l
                D['hfull'] = xT if l == 0 else h1
                ka_phase(P, D, ps, has_ln0=(l == 0))
                P.barrier(); P.flush()
            with contextlib.ExitStack() as s_b:
                P.stack = s_b; P.suffix = '_b%d' % l
                kb_phase1(P, D, ps, l)
                P.flush()
            with contextlib.ExitStack() as s_b2:
                P.stack = s_b2; P.suffix = '_d%d' % l
                kb_phase2(P, D, ps, l)
                P.flush()
            with contextlib.ExitStack() as s_c:
                P.stack = s_c; P.suffix = '_c%d' % l
                D['hT'] = h0 if l == 0 else h1
                D['out'] = h1 if l == 0 else outT
                kc_phase(P, D, ps)
                P.barrier()
                if l == 1:
                    P.emit()
                else:
                    P.flush()
    return nc


_PROG = {}


def _host_inputs(inp, b):
    m = {'xT': np.ascontiguousarray(inp['x'][b].T)}
    cs1, tw, kx = fft_consts2()
    rc = rwkv_consts()
    m.update({'d64': ka_consts(), 'cs1': cs1, 'tw': tw, 'kx': kx, 'ident': rc['ident'], 'rmask': rc['rmask'], 'bones1': rc['bones1'], 'cmask': rc['cmask']})
    for l in range(2):
        sfx = '_%d' % l
        ka = ka_inputs(inp, l, None, (1.0, 1.0))
        for k in ('vecsA', 'w_in', 'wfT', 'w1', 'a1', 'g1', 'w2', 'a2', 'g2'):
            m[k + sfx] = np.ascontiguousarray(ka[k], dtype=np.float32)
        m['cw' + sfx] = np.ascontiguousarray(inp['conv_w'][l].T)
        s5 = [s5_host(inp, l, q) for q in range(4)]
        for k in ('s5par', 's5b', 's5c', 's5d'):
            m[k + sfx] = np.concatenate([s[k] for s in s5], 0)
        m['rpar' + sfx] = np.concatenate([rwkv_params(inp, l, q) for q in range(4)], 0)
        m['vecs' + sfx] = kc_vecs(inp, l)
        m['glu_w' + sfx] = inp['s5_glu_w'][l]; m['w_out' + sfx] = inp['w_out'][l]
        m['fw1' + sfx] = inp['ffn_w1'][l]; m['fw3' + sfx] = inp['ffn_w3'][l]; m['fw2' + sfx] = inp['ffn_w2'][l]
    return m


def rwkv_params(inp, l, q):
    par = np.zeros((128, 16), np.float32)
    for d in range(2):
        rows = slice(d * 64, (d + 1) * 64)
        for j in range(3):
            par[rows, j] = inp['rwkv_mu_rkv'][l, d, j * 256 + 64 * q:j * 256 + 64 * q + 64]
        par[rows, 3] = inp['rwkv_k_k'][l, 64 * q:64 * q + 64]
        par[rows, 4] = inp['rwkv_k_a'][l, 64 * q:64 * q + 64]
        par[rows, 5] = inp['rwkv_r_k'][l, q]
    return par


def kernel(**inp):
    inp = {k: np.asarray(v) for k, v in inp.items()}
    if 'nc' not in _PROG:
        _PROG['nc'] = build_fused()
    per_b = [_host_inputs(inp, b) for b in range(2)]
    maps = [per_b[c // 4] for c in range(8)]
    res = run_bass_kernel_spmd(_PROG['nc'], maps, core_ids=list(range(8))).results
    out = np.empty((2, SEQ, 1024), np.float32)
    for c in range(8):
        b, qt = c // 4, c % 4
        out[b, qt * 2048:(qt + 1) * 2048, :] = res[c]['outT'][:, qt * 2048:(qt + 1) * 2048].T
    return out
```
